# Optimizing a Trainium2 kernel written in Bass

```python
import jax, jax.numpy as jnp
from jax import lax
import numpy as np

D_MODEL = 2048
BATCH = 2
SEQ = 4096
DEPTH = 2

CHUNK = 64
QBLOCK = 128
N_META = 16
PAD = QBLOCK - N_META
N_A = DEPTH // 2
N_B = DEPTH - N_A
GLA_HEADS = 4
GLA_KD = D_MODEL // 2
GLA_VD = D_MODEL
GLA_DK = GLA_KD // GLA_HEADS
GLA_DV = GLA_VD // GLA_HEADS
GATE_RANK = 16
GATE_TAU = 16.0
SB_HEAD_DIM = 128
SB_HEADS = D_MODEL // SB_HEAD_DIM
D_FF = -(-8 * D_MODEL // (3 * 256)) * 256
EPS = 1e-6

kernel_name = 'yoco_gla_stickbreaking_encoder'


def rms_norm(x, g):
    xf = x.astype(jnp.float32)
    y = xf * lax.rsqrt(jnp.mean(xf * xf, axis=-1, keepdims=True) + EPS)
    return (y * g.astype(jnp.float32)).astype(x.dtype)


def swiglu_ffn(h, g, w_gate, w_up, w_down):
    y = rms_norm(h, g)
    return (jax.nn.silu(y @ w_gate) * (y @ w_up)) @ w_down


def gla_mixer(h, valid, g_norm, w_in, w_gate_up, b_gate, g_onorm, w_out):
    B, L, _ = h.shape
    n_c = L // CHUNK
    y = rms_norm(h, g_norm)
    proj = y @ w_in
    q, k, v, g, a = jnp.split(proj, [GLA_KD, 2 * GLA_KD, 2 * GLA_KD + GLA_VD, 2 * GLA_KD + 2 * GLA_VD], axis=-1)
    k = k * valid[None, :, None]
    la = jax.nn.log_sigmoid((a @ w_gate_up + b_gate).astype(jnp.float32)) / GATE_TAU

    def heads(t, dh):
        return t.astype(jnp.float32).reshape(B, n_c, CHUNK, GLA_HEADS, dh).transpose(0, 1, 3, 2, 4)

    q = heads(q, GLA_DK) * GLA_DK ** -0.5
    k = heads(k, GLA_DK)
    v = heads(v, GLA_DV)
    bcum = jnp.cumsum(heads(la, GLA_DK), axis=3)
    ref = bcum[:, :, :, CHUNK // 2 - 1:CHUNK // 2, :]
    s_lo = jnp.einsum('bnhtd,bnhsd->bnhts', q * jnp.exp(bcum - ref), k * jnp.exp(ref - bcum))
    s_up = jnp.einsum('bnhtd,bnhsd->bnhts', q * jnp.exp(ref - bcum), k * jnp.exp(bcum - ref))
    idx = jnp.arange(CHUNK)
    scores = jnp.where(idx[:, None] >= idx[None, :], s_lo, s_up)
    o_intra = jnp.einsum('bnhts,bnhsv->bnhtv', scores, v)
    b_last = bcum[:, :, :, -1:, :]
    k_state = k * jnp.exp(b_last - bcum)
    q_inter = q * jnp.exp(bcum)
    chunk_decay = jnp.exp(b_last[:, :, :, 0, :])

    def step(S, inp):
        qc, kc, vc, dc = inp
        o_c = jnp.einsum('bhtd,bhdv->bhtv', qc, S)
        S_new = dc[..., None] * S + jnp.einsum('bhsd,bhsv->bhdv', kc, vc)
        return S_new, o_c

    xs = (jnp.moveaxis(q_inter, 1, 0), jnp.moveaxis(k_state, 1, 0), jnp.moveaxis(v, 1, 0), jnp.moveaxis(chunk_decay, 1, 0))
    S0 = jnp.zeros((B, GLA_HEADS, GLA_DK, GLA_DV), jnp.float32)
    _, o_inter = lax.scan(step, S0, xs)
    o = o_intra + jnp.moveaxis(o_inter, 0, 1)
    o = rms_norm(o, g_onorm)
    o = o.transpose(0, 1, 3, 2, 4).reshape(B, L, GLA_VD)
    o = o * jax.nn.silu(g.astype(jnp.float32))
    return o.astype(h.dtype) @ w_out


def stick_breaking_mixer(h, k_sh, v_sh, key_valid, g_norm, w_q, g_q, w_o):
    B, L, _ = h.shape
    y = rms_norm(h, g_norm)
    q = rms_norm((y @ w_q).reshape(B, L, SB_HEADS, SB_HEAD_DIM), g_q).astype(jnp.float32) * SB_HEAD_DIM ** -0.5
    pos = jnp.arange(L)
    outs = []
    for blk in range(L // QBLOCK):
        q0 = blk * QBLOCK
        kend = q0 + QBLOCK
        z = jnp.einsum('bqhd,bkhd->bhqk', q[:, q0:kend], k_sh[:, :kend])
        valid = (pos[None, :kend] < pos[q0:kend, None]) & key_valid[None, :kend]
        l_fail = jnp.where(valid, jax.nn.log_sigmoid(-z), 0.0)
        suffix = lax.cumsum(l_fail, axis=3, reverse=True) - l_fail
        att = jnp.where(valid, jnp.exp(jax.nn.log_sigmoid(z) + suffix), 0.0)
        outs.append(jnp.einsum('bhqk,bkhd->bqhd', att, v_sh[:, :kend]))
    o = jnp.concatenate(outs, axis=1).reshape(B, L, D_MODEL)
    return o.astype(h.dtype) @ w_o


def setup_inputs(seed: int = 0) -> dict:
    key = jax.random.key(seed)
    ks = jax.random.split(key, 19)
    f32 = jnp.float32

    def dense(k, shape, fan_in):
        return jax.random.normal(k, shape, f32) * fan_in ** -0.5

    def gain(k, shape):
        return 1.0 + 0.02 * jax.random.normal(k, shape, f32)

    n_in_a = 2 * GLA_KD + 2 * GLA_VD + GATE_RANK
    return {
        'x': jax.random.normal(ks[0], (BATCH, SEQ, D_MODEL), f32),
        'meta_tokens': jax.random.normal(ks[1], (N_META, D_MODEL), f32),
        'g_norm_a': gain(ks[2], (N_A, D_MODEL)),
        'w_in_a': dense(ks[3], (N_A, D_MODEL, n_in_a), D_MODEL),
        'w_gate_up_a': dense(ks[4], (N_A, GATE_RANK, GLA_KD), GATE_RANK),
        'b_gate_a': 0.1 * jax.random.normal(ks[5], (N_A, GLA_KD), f32),
        'g_onorm_a': gain(ks[6], (N_A, GLA_DV)),
        'w_out_a': dense(ks[7], (N_A, GLA_VD, D_MODEL), GLA_VD),
        'g_kv_norm': gain(ks[8], (D_MODEL,)),
        'w_kv': dense(ks[9], (D_MODEL, 2 * D_MODEL), D_MODEL),
        'g_k': gain(ks[10], (SB_HEAD_DIM,)),
        'g_norm_b': gain(ks[11], (N_B, D_MODEL)),
        'w_q_b': dense(ks[12], (N_B, D_MODEL, D_MODEL), D_MODEL),
        'g_q_b': gain(ks[13], (N_B, SB_HEAD_DIM)),
        'w_o_b': dense(ks[14], (N_B, D_MODEL, D_MODEL), D_MODEL),
        'g_ffn_norm': gain(ks[15], (DEPTH, D_MODEL)),
        'w_ffn_gate': dense(ks[16], (DEPTH, D_MODEL, D_FF), D_MODEL),
        'w_ffn_up': dense(ks[17], (DEPTH, D_MODEL, D_FF), D_MODEL),
        'w_ffn_down': dense(ks[18], (DEPTH, D_FF, D_MODEL), D_FF),
    }


def reference(x, meta_tokens, g_norm_a, w_in_a, w_gate_up_a, b_gate_a, g_onorm_a, w_out_a,
              g_kv_norm, w_kv, g_k, g_norm_b, w_q_b, g_q_b, w_o_b,
              g_ffn_norm, w_ffn_gate, w_ffn_up, w_ffn_down):
    B = x.shape[0]
    h = jnp.concatenate([
        jnp.zeros((B, PAD, D_MODEL), x.dtype),
        jnp.broadcast_to(meta_tokens.astype(x.dtype)[None], (B, N_META, D_MODEL)),
        x,
    ], axis=1)
    L = h.shape[1]
    key_valid = jnp.arange(L) >= PAD
    valid = key_valid.astype(x.dtype)
    k_sh = v_sh = None
    for layer in range(DEPTH):
        if layer < N_A:
            h = h + gla_mixer(h, valid, g_norm_a[layer], w_in_a[layer], w_gate_up_a[layer],
                              b_gate_a[layer], g_onorm_a[layer], w_out_a[layer])
        else:
            if layer == N_A:
                kv = rms_norm(h, g_kv_norm) @ w_kv
                k_sh, v_sh = jnp.split(kv, 2, axis=-1)
                k_sh = rms_norm(k_sh.reshape(B, L, SB_HEADS, SB_HEAD_DIM), g_k).astype(jnp.float32)
                v_sh = v_sh.reshape(B, L, SB_HEADS, SB_HEAD_DIM).astype(jnp.float32)
            j = layer - N_A
            h = h + stick_breaking_mixer(h, k_sh, v_sh, key_valid, g_norm_b[j], w_q_b[j], g_q_b[j], w_o_b[j])
        h = h + swiglu_ffn(h, g_ffn_norm[layer], w_ffn_gate[layer], w_ffn_up[layer], w_ffn_down[layer])
    return h[:, PAD + N_META:]
```

```python
import numpy as np
import ml_dtypes
from concourse.bass_utils import run_bass_kernel_spmd
import numpy as np
import concourse.bass as bass
import concourse.mybir as mybir

F32 = mybir.dt.float32
BF16 = mybir.dt.bfloat16
AF = mybir.ActivationFunctionType
ALU = mybir.AluOpType
AX = mybir.AxisListType

ENGS = ("pe", "act", "dve", "pool", "sp")


class Op:
    __slots__ = ("idx", "eng", "fn", "reads", "writes", "dma", "deps", "signal", "count",
                 "pre_wait", "group", "waits")

    def __init__(self, idx, eng, fn, reads, writes, dma):
        self.idx = idx
        self.eng = eng
        self.fn = fn
        self.reads = tuple(reads)
        self.writes = tuple(writes)
        self.dma = dma
        self.deps = []
        self.signal = False
        self.count = None
        self.pre_wait = None
        self.group = None
        self.waits = []


class Prog:
    def __init__(self, nc):
        self.nc = nc
        self.ops = []
        self.tri = 0

    def add(self, eng, fn, reads=(), writes=(), dma=None):
        op = Op(len(self.ops), eng, fn, reads, writes, dma)
        self.ops.append(op)
        return op

    def dma(self, eng, slot, out, in_, reads=(), writes=()):
        def fn(e, out=out, in_=in_):
            return e.dma_start(out=out, in_=in_)
        return self.add(eng, fn, reads, writes, dma=slot)

    def next_tri(self):
        t = self.tri
        self.tri ^= 1
        return t

    def build(self):
        nc = self.nc
        ops = self.ops
        last_writer = {}
        readers = {}
        dcount = {}
        dgroup = {}
        dclosed = {}
        gend = {}
        for op in ops:
            deps = set()
            for k in op.reads:
                if k in last_writer:
                    deps.add(last_writer[k])
            for k in op.writes:
                if k in last_writer:
                    deps.add(last_writer[k])
                for r in readers.get(k, ()):
                    deps.add(r)
            deps.discard(op.idx)
            op.deps = sorted(deps)
            if op.dma is not None:
                s = op.dma
                if s not in dcount:
                    dcount[s] = 0
                    dgroup[s] = 0
                    dclosed[s] = False
                if dclosed[s]:
                    gend[(s, dgroup[s])] = dcount[s]
                    op.pre_wait = (("dma", s), dcount[s])
                    dgroup[s] += 1
                    dclosed[s] = False
                dcount[s] += 16
                op.group = dgroup[s]
            for di in op.deps:
                p = ops[di]
                if p.dma is not None:
                    s = p.dma
                    if p.group == dgroup[s]:
                        dclosed[s] = True
                        val = dcount[s]
                        gend[(s, p.group)] = val
                    else:
                        val = gend[(s, p.group)]
                    op.waits.append((("dma", s), val))
                else:
                    if p.eng == "pe" and op.eng == "pe" and op.dma is None:
                        continue
                    p.signal = True
                    op.waits.append((("eng", p.eng), di))
            for k in op.reads:
                readers.setdefault(k, []).append(op.idx)
            for k in op.writes:
                last_writer[k] = op.idx
                readers[k] = []
        cnt = {e: 0 for e in ENGS}
        for op in ops:
            if op.dma is None and op.signal:
                cnt[op.eng] += 1
                op.count = cnt[op.eng]
        self.max_counts = dict(cnt)
        sems = {}
        for e in ENGS:
            if cnt[e] > 0:
                sems[("eng", e)] = nc.alloc_semaphore("s_" + e)
        for s in dcount:
            sems[("dma", s)] = nc.alloc_semaphore("d_" + str(s))
        self.sems = sems
        per_eng = {e: [op for op in ops if op.eng == e] for e in ENGS}

        def emit(ename, eng):
            seen = {}
            for op in per_eng[ename]:
                ws = []
                if op.pre_wait is not None:
                    ws.append(op.pre_wait)
                for (sk, v) in op.waits:
                    if sk[0] == "eng":
                        v = ops[v].count
                    ws.append((sk, v))
                for (sk, v) in ws:
                    if v <= seen.get(sk, 0):
                        continue
                    seen[sk] = v
                    eng.wait_ge(sems[sk], v)
                if op.fn is None:
                    continue
                ins = op.fn(eng)
                if op.dma is not None:
                    ins.then_inc(sems[("dma", op.dma)], 16)
                elif op.signal:
                    ins.then_inc(sems[("eng", ename)], 1)

        with nc.Block() as block:
            @block.tensor
            def _(e):
                emit("pe", e)

            @block.scalar
            def _(e):
                emit("act", e)

            @block.vector
            def _(e):
                emit("dve", e)

            @block.gpsimd
            def _(e):
                emit("pool", e)

            @block.sync
            def _(e):
                emit("sp", e)


T = 1056
TG = 352
NTG = 3
D = 2048
KT_D = 16
EPS = 1e-6


class DenseCtx:
    def __init__(self, nc, P, wb=256, max_kt=16, n_wslots=3):
        self.nc = nc
        self.P = P
        self.wb = wb
        self.psum = nc.alloc_psum_tensor("psum_all", [128, 8, 512], F32)
        self.wslots = [nc.alloc_sbuf_tensor(f"wslot{i}", [128, max_kt, wb], BF16) for i in range(n_wslots)]
        self.wi = 0
        self.ones = nc.alloc_sbuf_tensor("ones_f32", [128, 128], F32)
        self.sq = [nc.alloc_sbuf_tensor(f"sq{i}", [128, T], F32) for i in range(2)]
        self.sqi = 0
        self.rstd = nc.alloc_sbuf_tensor("rstd_bc", [128, T], F32)
        self.evi = 0
        P.add("pool", lambda e: e.memset(self.ones[:, :], 1.0), writes=[("ones",)])

    def tri_view(self, t):
        return self.psum[:, 3 * t:3 * t + 3, 0:TG]

    def tri_key(self, t):
        return ("ps", t)

    def next_wslot(self):
        i = self.wi
        self.wi = (self.wi + 1) % len(self.wslots)
        return i


def v3(ap):
    return ap.rearrange("p (g t) -> p g t", g=NTG)


def rms_stats(C, hT, hkey):
    P = C.P
    nc = C.nc
    t = P.next_tri()
    for kt in range(KT_D):
        si = C.sqi
        C.sqi ^= 1
        sq = C.sq[si]
        P.add("act", lambda e, sq=sq, kt=kt: e.activation(out=sq[:, :], in_=hT[:, kt, :], func=AF.Square),
              reads=[hkey(kt)], writes=[("sq", si)])

        def mm(e, sq=sq, kt=kt, t=t):
            ins = None
            for g in range(NTG):
                ins = e.matmul(C.psum[:, 3 * t + g, 0:TG], lhsT=C.ones[:, :], rhs=sq[:, g * TG:(g + 1) * TG],
                               start=(kt == 0), stop=(kt == KT_D - 1))
            return ins
        P.add("pe", mm, reads=[("sq", si), ("ones",)], writes=[C.tri_key(t)])
    P.add("act", lambda e: e.activation(out=v3(C.rstd[:, :]), in_=C.tri_view(t), func=AF.Sqrt,
                                        scale=1.0 / D, bias=EPS),
          reads=[C.tri_key(t)], writes=[("rstd",)])
    P.add("dve", lambda e: e.reciprocal(out=C.rstd[:, :], in_=C.rstd[:, :]),
          reads=[("rstd",)], writes=[("rstd",)])


def apply_norm(C, hT, hkey, gcol, gkey, yT, ykey):
    P = C.P
    for kt in range(KT_D):
        P.add("dve", lambda e, kt=kt: e.scalar_tensor_tensor(out=yT[:, kt, :], in0=hT[:, kt, :],
                                                             scalar=gcol[:, kt:kt + 1], in1=C.rstd[:, :],
                                                             op0=ALU.mult, op1=ALU.mult),
              reads=[hkey(kt), gkey, ("rstd",)], writes=[ykey(kt)])


def dense(C, name, xT, xkey, KT, w_dram, row0, col0, ncols, epilogue):
    P = C.P
    wb = C.wb
    nblocks = (ncols + wb - 1) // wb
    nt = 0
    for b in range(nblocks):
        c0 = col0 + b * wb
        cw = min(wb, col0 + ncols - c0)
        ws = C.next_wslot()
        wt = C.wslots[ws]
        src = w_dram[row0:row0 + KT * 128, c0:c0 + cw].rearrange("(kt p) n -> p kt n", p=128)
        P.dma("pool", f"w{ws}", out=wt[:, 0:KT, 0:cw], in_=src, writes=[("w", ws)])
        for j in range((cw + 127) // 128):
            rows = min(128, cw - j * 128)
            t = P.next_tri()

            def mm(e, wt=wt, j=j, rows=rows, t=t):
                ins = None
                for kt in range(KT):
                    for g in range(NTG):
                        ins = e.matmul(C.psum[0:rows, 3 * t + g, 0:TG], lhsT=wt[:, kt, j * 128:j * 128 + rows],
                                       rhs=xT(kt)[:, g * TG:(g + 1) * TG], start=(kt == 0), stop=(kt == KT - 1))
                return ins
            P.add("pe", mm, reads=[("w", ws)] + [xkey(kt) for kt in range(KT)], writes=[C.tri_key(t)])
            epilogue(nt, rows, t)
            nt += 1


N_IN = 6160


def build_l1():
    nc = bass.Bass("TRN2", target_bir_lowering=False)
    hT_d = nc.dram_tensor("hT", [D, T], F32, kind="ExternalInput")
    w_d = nc.dram_tensor("w_in", [D, N_IN], F32, kind="ExternalInput")
    g_d = nc.dram_tensor("gcol", [128, 16], F32, kind="ExternalInput")
    proj_d = nc.dram_tensor("projT", [6144, T], BF16, kind="ExternalOutput")
    a_d = nc.dram_tensor("aT", [16, T], F32, kind="ExternalOutput")

    P = Prog(nc)
    C = DenseCtx(nc, P)
    hT = nc.alloc_sbuf_tensor("hT_sb", [128, KT_D, T], F32)
    yT = nc.alloc_sbuf_tensor("yT_sb", [128, KT_D, T], BF16)
    gcol = nc.alloc_sbuf_tensor("gcol_sb", [128, 16], F32)
    outs = [nc.alloc_sbuf_tensor(f"o{i}", [128, T], BF16) for i in range(4)]
    a_sb = nc.alloc_sbuf_tensor("a_sb", [16, T], F32)

    hkey = lambda kt: ("h", kt)
    ykey = lambda kt: ("y", kt)
    P.dma("sp", "g", out=gcol[:, :], in_=g_d[:, :], writes=[("g",)])
    for q in range(4):
        P.dma("sp", "hin", out=hT[:, 4 * q:4 * q + 4, :],
              in_=hT_d[512 * q:512 * (q + 1), :].rearrange("(kt p) t -> p kt t", p=128),
              writes=[hkey(kt) for kt in range(4 * q, 4 * q + 4)])
    rms_stats(C, hT, hkey)
    apply_norm(C, hT, hkey, gcol, ("g",), yT, ykey)

    state = {"i": 0}

    def epi(nt, rows, t):
        if nt < 48:
            i = state["i"] % 4
            state["i"] += 1
            o = outs[i]
            eng = "act" if nt % 2 == 0 else "dve"
            if eng == "act":
                P.add("act", lambda e, o=o, t=t: e.activation(out=v3(o[:, :]), in_=C.tri_view(t), func=AF.Copy),
                      reads=[C.tri_key(t)], writes=[("o", i)])
            else:
                P.add("dve", lambda e, o=o, t=t: e.tensor_copy(out=v3(o[:, :]), in_=C.tri_view(t)),
                      reads=[C.tri_key(t)], writes=[("o", i)])
            P.dma("sp", f"o{i}", out=proj_d[nt * 128:(nt + 1) * 128, :], in_=o[:, :], reads=[("o", i)],
                  writes=[("projout", nt)])
        else:
            P.add("dve", lambda e, t=t: e.tensor_copy(out=v3(a_sb[:, :]), in_=C.psum[0:16, 3 * t:3 * t + 3, 0:TG]),
                  reads=[C.tri_key(t)], writes=[("a",)])
            P.dma("sp", "aout", out=a_d[:, :], in_=a_sb[:, :], reads=[("a",)], writes=[("aout",)])

    dense(C, "w_in", lambda kt: yT[:, kt, :], ykey, KT_D, w_d, 0, 0, N_IN, epi)
    P.add("sp", None, reads=[("projout", nt) for nt in range(48)] + [("aout",)])
    P.build()
    return nc, P


DFF = 5632
SCS = [12, 12, 12, 8]


def build_l35(kind):
    nc = bass.Bass("TRN2", target_bir_lowering=False)
    hT_d = nc.dram_tensor("hT", [D, T], F32, kind="ExternalInput")
    oT_d = nc.dram_tensor("oT", [D, T], BF16, kind="ExternalInput")
    wmix_d = nc.dram_tensor("w_mix", [D, D], F32, kind="ExternalInput")
    wg_d = nc.dram_tensor("w_gate", [D, DFF], F32, kind="ExternalInput")
    wu_d = nc.dram_tensor("w_up", [D, DFF], F32, kind="ExternalInput")
    wd_d = nc.dram_tensor("w_down", [DFF, D], F32, kind="ExternalInput")
    g_d = nc.dram_tensor("gcols", [128, 48], F32, kind="ExternalInput")
    hout_d = nc.dram_tensor("hout", [D, T], F32, kind="ExternalOutput")
    if kind == "L3":
        wkv_d = nc.dram_tensor("w_kv", [D, 2 * D], F32, kind="ExternalInput")
        wq_d = nc.dram_tensor("w_q", [D, D], F32, kind="ExternalInput")
        kv_d = nc.dram_tensor("kvT", [2 * D, T], BF16, kind="ExternalOutput")
        q_d = nc.dram_tensor("qT", [D, T], BF16, kind="ExternalOutput")

    P = Prog(nc)
    C = DenseCtx(nc, P, n_wslots=4)
    hT = nc.alloc_sbuf_tensor("hT_sb", [128, KT_D, T], F32)
    yT = nc.alloc_sbuf_tensor("yT_sb", [128, KT_D, T], BF16)
    uT = nc.alloc_sbuf_tensor("uT_sb", [128, 12, T], BF16)
    gcols = nc.alloc_sbuf_tensor("gcols_sb", [128, 48], F32)
    sg = [nc.alloc_sbuf_tensor(f"sg{i}", [128, T], F32) for i in range(2)]
    outs = [nc.alloc_sbuf_tensor(f"o{i}", [128, T], BF16) for i in range(4)]

    hkey = lambda kt: ("h", kt)
    ykey = lambda kt: ("y", kt)
    ukey = lambda kt: ("u", kt)
    P.dma("sp", "g", out=gcols[:, :], in_=g_d[:, :], writes=[("g",)])
    for q in range(4):
        P.dma("sp", "oin", out=yT[:, 4 * q:4 * q + 4, :],
              in_=oT_d[512 * q:512 * (q + 1), :].rearrange("(kt p) t -> p kt t", p=128),
              writes=[ykey(kt) for kt in range(4 * q, 4 * q + 4)])
    for q in range(4):
        P.dma("sp", "hin", out=hT[:, 4 * q:4 * q + 4, :],
              in_=hT_d[512 * q:512 * (q + 1), :].rearrange("(kt p) t -> p kt t", p=128),
              writes=[hkey(kt) for kt in range(4 * q, 4 * q + 4)])

    def epi_resid(nt, rows, t):
        P.add("dve", lambda e, nt=nt, t=t: e.tensor_tensor(out=v3(hT[:, nt, :]), in0=v3(hT[:, nt, :]),
                                                           in1=C.tri_view(t), op=ALU.add),
              reads=[C.tri_key(t), hkey(nt)], writes=[hkey(nt)])

    dense(C, "w_mix", lambda kt: yT[:, kt, :], ykey, KT_D, wmix_d, 0, 0, D, epi_resid)

    rms_stats(C, hT, hkey)
    apply_norm(C, hT, hkey, gcols[:, 0:16], ("g",), yT, ykey)
    sgi = [0]
    f_nt = 0
    wb = C.wb
    for sc_n in SCS:
        for blk in range(sc_n // 2):
            c0 = (f_nt + 2 * blk) * 128
            wsg = C.next_wslot()
            P.dma("pool", f"w{wsg}", out=C.wslots[wsg][:, 0:KT_D, 0:wb],
                  in_=wg_d[:, c0:c0 + wb].rearrange("(kt p) n -> p kt n", p=128), writes=[("w", wsg)])
            wsu = C.next_wslot()
            P.dma("pool", f"w{wsu}", out=C.wslots[wsu][:, 0:KT_D, 0:wb],
                  in_=wu_d[:, c0:c0 + wb].rearrange("(kt p) n -> p kt n", p=128), writes=[("w", wsu)])
            for j in range(2):
                jj = 2 * blk + j

                def mm(e, ws, j, t):
                    wt = C.wslots[ws]
                    ins = None
                    for kt in range(KT_D):
                        for g in range(NTG):
                            ins = e.matmul(C.psum[:, 3 * t + g, 0:TG], lhsT=wt[:, kt, j * 128:(j + 1) * 128],
                                           rhs=yT[:, kt, g * TG:(g + 1) * TG], start=(kt == 0), stop=(kt == KT_D - 1))
                    return ins
                tg_ = P.next_tri()
                P.add("pe", lambda e, ws=wsg, j=j, t=tg_: mm(e, ws, j, t),
                      reads=[("w", wsg)] + [ykey(kt) for kt in range(KT_D)], writes=[C.tri_key(tg_)])
                si = sgi[0]
                sgi[0] ^= 1
                P.add("act", lambda e, si=si, t=tg_: e.activation(out=v3(sg[si][:, :]), in_=C.tri_view(t), func=AF.Silu),
                      reads=[C.tri_key(tg_)], writes=[("sg", si)])
                tu_ = P.next_tri()
                P.add("pe", lambda e, ws=wsu, j=j, t=tu_: mm(e, ws, j, t),
                      reads=[("w", wsu)] + [ykey(kt) for kt in range(KT_D)], writes=[C.tri_key(tu_)])
                P.add("dve", lambda e, si=si, t=tu_, jj=jj: e.tensor_tensor(out=v3(uT[:, jj, :]), in0=v3(sg[si][:, :]),
                                                                            in1=C.tri_view(t), op=ALU.mult),
                      reads=[C.tri_key(tu_), ("sg", si)], writes=[ukey(jj)])
        dense(C, "w_down", lambda kt: uT[:, kt, :], ukey, sc_n, wd_d, f_nt * 128, 0, D, epi_resid)
        f_nt += sc_n

    for q in range(4):
        P.dma("sp", "hout", out=hout_d[512 * q:512 * (q + 1), :].rearrange("(kt p) t -> p kt t", p=128),
              in_=hT[:, 4 * q:4 * q + 4, :], reads=[hkey(kt) for kt in range(4 * q, 4 * q + 4)],
              writes=[("hout", q)])
    fin = [("hout", q) for q in range(4)]

    if kind == "L3":
        state = {"i": 0}

        def mk_epi(dst, tag):
            def epi(nt, rows, t):
                i = state["i"] % 4
                state["i"] += 1
                o = outs[i]
                if nt % 2 == 0:
                    P.add("act", lambda e, o=o, t=t: e.activation(out=v3(o[:, :]), in_=C.tri_view(t), func=AF.Copy),
                          reads=[C.tri_key(t)], writes=[("o", i)])
                else:
                    P.add("dve", lambda e, o=o, t=t: e.tensor_copy(out=v3(o[:, :]), in_=C.tri_view(t)),
                          reads=[C.tri_key(t)], writes=[("o", i)])
                P.dma("sp", f"o{i}", out=dst[nt * 128:(nt + 1) * 128, :], in_=o[:, :], reads=[("o", i)],
                      writes=[(tag, nt)])
                fin.append((tag, nt))
            return epi
        rms_stats(C, hT, hkey)
        apply_norm(C, hT, hkey, gcols[:, 16:32], ("g",), yT, ykey)
        dense(C, "w_kv", lambda kt: yT[:, kt, :], ykey, KT_D, wkv_d, 0, 0, 2 * D, mk_epi(kv_d, "kvout"))
        apply_norm(C, hT, hkey, gcols[:, 32:48], ("g",), yT, ykey)
        dense(C, "w_q", lambda kt: yT[:, kt, :], ykey, KT_D, wq_d, 0, 0, D, mk_epi(q_d, "qout"))

    P.add("sp", None, reads=fin)
    P.build()
    return nc, P


L = 4224
NB = 33
GB = 11
NG = 3
GT = GB * 128
GC = GB * 2
DK = 256
DV = 512
EPS = 1e-6


def build_l2():
    nc = bass.Bass("TRN2", target_bir_lowering=False)
    qT_d = nc.dram_tensor("qT", [DK, L], BF16, kind="ExternalInput")
    kT_d = nc.dram_tensor("kT", [DK, L], BF16, kind="ExternalInput")
    v_d = nc.dram_tensor("v", [L, DV], BF16, kind="ExternalInput")
    gT_d = nc.dram_tensor("gT", [DV, L], BF16, kind="ExternalInput")
    aT_d = nc.dram_tensor("aT", [16, L], F32, kind="ExternalInput")
    wgu_d = nc.dram_tensor("wgu", [16, DK], F32, kind="ExternalInput")
    cols_d = nc.dram_tensor("cols", [128, 8], F32, kind="ExternalInput")
    og_d = nc.dram_tensor("ogT", [DV, L], BF16, kind="ExternalOutput")

    P = Prog(nc)
    ps = nc.alloc_psum_tensor("ps", [128, 7, 512], F32)
    pst = nc.alloc_psum_tensor("pst", [128, 1024], BF16)
    B_Z, B_UP, B_U0, B_U1, B_O0, B_O1, B_ST = 0, 1, 2, 3, 4, 5, 6

    sb = lambda name, shape, dt: nc.alloc_sbuf_tensor(name + "_sb", shape, dt)
    ones = sb("ones", [128, 128], F32)
    ident = sb("ident", [128, 128], BF16)
    mlo = sb("mlo", [64, 64], F32)
    mup = sb("mup", [64, 64], F32)
    cols = sb("cols", [128, 8], F32)
    wgu = sb("wgu", [16, DK], F32)
    aT = sb("aT", [16, GT], F32)
    qT = sb("qT", [128, 2, GT], BF16)
    kT = sb("kT", [128, 2, GT], BF16)
    v64 = sb("v64", [64, GC, DV], BF16)
    gs = sb("gs", [128, 4, GT], BF16)
    et = sb("et", [128, GT], F32)
    cc = sb("cc", [128, GT], F32)
    dd = sb("dd", [128, GT], F32)
    ep = sb("ep", [128, GT], F32)
    em = sb("em", [128, GT], F32)
    rmask = sb("rmask", [128, GT], F32)
    al = sb("al", [128, GC], F32)
    be = sb("be", [128, GC], F32)
    dec = sb("dec", [128, 2, GC], F32)
    tsm = sb("tsm", [128, GC], F32)
    qa = sb("qa", [128, 2, GT], BF16)
    qb = sb("qb", [128, 2, GT], BF16)
    ka = sb("ka", [128, 2, GT], BF16)
    kb = sb("kb", [128, 2, GT], BF16)
    qi = sb("qi", [128, 2, GT], BF16)
    ksT = sb("ksT", [128, 2, GT], BF16)
    ks64 = sb("ks64", [64, GC, DK], BF16)
    sc = sb("sc", [64, GC, 64], BF16)
    t1 = sb("t1", [64, 8, 64], F32)
    t2 = sb("t2", [64, 8, 64], F32)
    S = sb("S", [128, 2, DV], F32)
    Sbf = sb("Sbf", [128, 2, DV], BF16)
    sq = sb("sq", [128, 4, 128], F32)
    rs = sb("rs", [128, 128], F32)
    tmp = sb("tmp", [128, 4, 128], F32)

    P.add("pool", lambda e: e.memset(ones[:, :], 1.0), writes=["ones"])
    P.add("pool", lambda e: e.memset(ident[:, :], 1.0), writes=["ident"])
    P.add("pool", lambda e: e.affine_select(out=ident[:, :], in_=ident[:, :], pattern=[[-1, 128]],
                                            compare_op=ALU.is_equal, fill=0.0, base=0, channel_multiplier=1),
          reads=["ident"], writes=["ident"])
    P.add("pool", lambda e: e.memset(mlo[:, :], 1.0), writes=["mlo"])
    P.add("pool", lambda e: e.affine_select(out=mlo[:, :], in_=mlo[:, :], pattern=[[1, 64]],
                                            compare_op=ALU.is_ge, fill=0.0, base=0, channel_multiplier=-1),
          reads=["mlo"], writes=["mlo"])
    P.add("pool", lambda e: e.memset(mup[:, :], 1.0), writes=["mup"])
    P.add("pool", lambda e: e.affine_select(out=mup[:, :], in_=mup[:, :], pattern=[[-1, 64]],
                                            compare_op=ALU.is_gt, fill=0.0, base=0, channel_multiplier=1),
          reads=["mup"], writes=["mup"])
    P.add("pool", lambda e: e.memset(rmask[:, :], 1.0), writes=["rmask"])
    P.add("pool", lambda e: e.memset(rmask[:, :].rearrange("p (c t) -> p c t", t=64)[:, :, 0:1], 0.0),
          reads=["rmask"], writes=["rmask"])
    P.add("pool", lambda e: e.memset(S[:, :, :], 0.0), writes=["S0", "S1"])
    P.add("pool", lambda e: e.memset(Sbf[:, :, :], 0.0), writes=["Sbf0", "Sbf1"])
    P.dma("sp", "c0", out=cols[:, :], in_=cols_d[:, :], writes=["cols"])
    P.dma("sp", "c0", out=wgu[:, :], in_=wgu_d[:, :], writes=["wgu"])
    negb = sb("negb", [128, 2], F32)
    P.add("pool", lambda e: e.tensor_scalar(out=negb[:, :], in0=cols[:, 0:2], scalar1=-1.0, scalar2=1.0, op0=ALU.mult, op1=ALU.mult),
          reads=["cols"], writes=["negb"])

    c3 = lambda ap: ap.rearrange("p (c t) -> p c t", t=64)
    fin = []
    obank = [0]

    for g in range(NG):
        tok0 = g * GT
        P.dma("sp", "in_a", out=aT[:, :], in_=aT_d[:, tok0:tok0 + GT], writes=["aT"])
        P.dma("sp", "in_q", out=qT[:, :, :], in_=qT_d[:, tok0:tok0 + GT].rearrange("(dt p) t -> p dt t", p=128),
              writes=["qT"])
        P.dma("sp", "in_k", out=kT[:, :, :], in_=kT_d[:, tok0:tok0 + GT].rearrange("(dt p) t -> p dt t", p=128),
              writes=["kT"])
        P.dma("sp", "in_v", out=v64[:, :, :], in_=v_d[tok0:tok0 + GT, :].rearrange("(c s) v -> s c v", s=64),
              writes=["v64"])
        P.dma("sp", "in_g", out=gs[:, :, :], in_=gT_d[:, tok0:tok0 + GT].rearrange("(vt p) t -> p vt t", p=128),
              writes=["gs"])
        for vt in range(4):
            P.add("act", lambda e, vt=vt: e.activation(out=gs[:, vt, :], in_=gs[:, vt, :], func=AF.Silu),
                  reads=["gs"], writes=["gs"])
            P.add("pool", lambda e, vt=vt: e.tensor_scalar(out=gs[:, vt, :], in0=gs[:, vt, :],
                                                           scalar1=cols[:, 2 + vt:3 + vt], scalar2=1.0, op0=ALU.mult, op1=ALU.mult),
                  reads=["gs", "cols"], writes=["gs"])
        for dt in range(2):
            nch = [(i * 512, min(512, GT - i * 512)) for i in range((GT + 511) // 512)]
            for (o0, w) in nch:
                P.add("pe", lambda e, o0=o0, w=w, dt=dt: e.matmul(ps[:, B_Z, 0:w], lhsT=wgu[:, dt * 128:(dt + 1) * 128],
                                                                  rhs=aT[:, o0:o0 + w], start=True, stop=True),
                      reads=["wgu", "aT"], writes=["bZ"])
                P.add("act", lambda e, o0=o0, w=w, dt=dt: e.activation(out=et[:, o0:o0 + w], in_=ps[:, B_Z, 0:w], func=AF.Exp,
                                                                       scale=-1.0, bias=negb[:, dt:dt + 1]),
                      reads=["bZ", "negb"], writes=["et"])
            P.add("act", lambda e: e.activation(out=cc[:, :], in_=et[:, :], func=AF.Ln, scale=1.0, bias=1.0),
                  reads=["et"], writes=["cc"])
            P.add("dve", lambda e: e.tensor_tensor_scan(out=cc[:, :], data0=rmask[:, :], data1=cc[:, :], initial=0.0,
                                                        op0=ALU.mult, op1=ALU.add),
                  reads=["cc", "rmask"], writes=["cc"])
            P.add("dve", lambda e: e.tensor_tensor(out=c3(dd[:, :]), in0=c3(cc[:, :]),
                                                   in1=c3(cc[:, :])[:, :, 31:32].to_broadcast([128, GC, 64]),
                                                   op=ALU.subtract),
                  reads=["cc"], writes=["dd"])
            P.add("act", lambda e: e.activation(out=ep[:, :], in_=dd[:, :], func=AF.Exp, scale=-1.0 / 16),
                  reads=["dd"], writes=["ep"])
            P.add("act", lambda e: e.activation(out=em[:, :], in_=dd[:, :], func=AF.Exp, scale=1.0 / 16),
                  reads=["dd"], writes=["em"])
            P.add("act", lambda e: e.activation(out=al[:, :], in_=c3(cc[:, :])[:, :, 31], func=AF.Exp, scale=-1.0 / 16),
                  reads=["cc"], writes=["al"])
            P.add("act", lambda e, dt=dt: e.activation(out=dec[:, dt, :], in_=c3(cc[:, :])[:, :, 63], func=AF.Exp,
                                                       scale=-1.0 / 16),
                  reads=["cc"], writes=[("dec", dt)])
            P.add("dve", lambda e: e.tensor_tensor(out=tsm[:, :], in0=c3(cc[:, :])[:, :, 63], in1=c3(cc[:, :])[:, :, 31],
                                                   op=ALU.subtract),
                  reads=["cc"], writes=["tsm"])
            P.add("act", lambda e: e.activation(out=be[:, :], in_=tsm[:, :], func=AF.Exp, scale=-1.0 / 16),
                  reads=["tsm"], writes=["be"])
            P.add("dve", lambda e, dt=dt: e.scalar_tensor_tensor(out=qa[:, dt, :], in0=qT[:, dt, :], scalar=1.0 / 16,
                                                                 in1=ep[:, :], op0=ALU.mult, op1=ALU.mult),
                  reads=["qT", "ep"], writes=[("qa", dt)])
            P.add("dve", lambda e, dt=dt: e.scalar_tensor_tensor(out=qb[:, dt, :], in0=qT[:, dt, :], scalar=1.0 / 16,
                                                                 in1=em[:, :], op0=ALU.mult, op1=ALU.mult),
                  reads=["qT", "em"], writes=[("qb", dt)])
            P.add("dve", lambda e, dt=dt: e.tensor_tensor(out=ka[:, dt, :], in0=kT[:, dt, :], in1=em[:, :], op=ALU.mult),
                  reads=["kT", "em"], writes=[("ka", dt)])
            P.add("dve", lambda e, dt=dt: e.tensor_tensor(out=kb[:, dt, :], in0=kT[:, dt, :], in1=ep[:, :], op=ALU.mult),
                  reads=["kT", "ep"], writes=[("kb", dt)])
            P.add("dve", lambda e, dt=dt: e.tensor_tensor(out=c3(qi[:, dt, :]), in0=c3(qa[:, dt, :]),
                                                          in1=al[:, :].unsqueeze(2).to_broadcast([128, GC, 64]), op=ALU.mult),
                  reads=[("qa", dt), "al"], writes=[("qi", dt)])
            P.add("dve", lambda e, dt=dt: e.tensor_tensor(out=c3(ksT[:, dt, :]), in0=c3(ka[:, dt, :]),
                                                          in1=be[:, :].unsqueeze(2).to_broadcast([128, GC, 64]), op=ALU.mult),
                  reads=[("ka", dt), "be"], writes=[("ksT", dt)])
        for c0 in range(0, GC, 4):
            n = min(4, GC - c0)

            def tr(e, c0=c0, n=n):
                ins = None
                for j in range(n):
                    for dt in range(2):
                        ins = e.transpose(out=pst[0:64, (j * 2 + dt) * 128:(j * 2 + dt + 1) * 128],
                                          in_=ksT[:, dt, (c0 + j) * 64:(c0 + j + 1) * 64], identity=ident[:, :])
                return ins
            P.add("pe", tr, reads=[("ksT", 0), ("ksT", 1), "ident"], writes=["pst"])
            P.add("act", lambda e, c0=c0, n=n: e.activation(
                out=ks64[:, c0:c0 + n, :], in_=pst[0:64, 0:n * 256].rearrange("p (c d) -> p c d", d=256), func=AF.Copy),
                reads=["pst"], writes=["ks64"])
        for c0 in range(0, GC, 8):
            n = min(8, GC - c0)

            def scm(e, c0=c0, n=n):
                ins = None
                for j in range(n):
                    cs = slice((c0 + j) * 64, (c0 + j + 1) * 64)
                    for dt in range(2):
                        e.matmul(ps[0:64, B_Z, j * 64:(j + 1) * 64], lhsT=ka[:, dt, cs], rhs=qa[:, dt, cs],
                                 start=(dt == 0), stop=(dt == 1))
                    for dt in range(2):
                        ins = e.matmul(ps[0:64, B_UP, j * 64:(j + 1) * 64], lhsT=kb[:, dt, cs], rhs=qb[:, dt, cs],
                                       start=(dt == 0), stop=(dt == 1))
                return ins
            P.add("pe", scm, reads=[("ka", 0), ("ka", 1), ("qa", 0), ("qa", 1), ("kb", 0), ("kb", 1), ("qb", 0), ("qb", 1)],
                  writes=["bZ", "bUP"])
            P.add("dve", lambda e, n=n: e.tensor_tensor(out=t1[:, 0:n, :], in0=c3(ps[0:64, B_Z, 0:n * 64]),
                                                        in1=mlo[:, :].unsqueeze(1).to_broadcast([64, n, 64]), op=ALU.mult),
                  reads=["bZ", "mlo"], writes=["t1"])
            P.add("dve", lambda e, n=n: e.tensor_tensor(out=t2[:, 0:n, :], in0=c3(ps[0:64, B_UP, 0:n * 64]),
                                                        in1=mup[:, :].unsqueeze(1).to_broadcast([64, n, 64]), op=ALU.mult),
                  reads=["bUP", "mup"], writes=["t2"])
            P.add("pool", lambda e, c0=c0, n=n: e.tensor_tensor(out=sc[:, c0:c0 + n, :], in0=t1[:, 0:n, :], in1=t2[:, 0:n, :],
                                                                op=ALU.add),
                  reads=["t1", "t2"], writes=["sc"])
        for blk in range(GB):
            ob = B_O0 + obank[0]
            obank[0] ^= 1
            okey = ("bO", ob)
            for h in range(2):
                c = blk * 2 + h
                cs = slice(c * 64, (c + 1) * 64)
                for dt in range(2):
                    P.add("pe", lambda e, c=c, dt=dt: e.matmul(ps[:, B_U0 + dt, :], lhsT=ks64[:, c, dt * 128:(dt + 1) * 128],
                                                               rhs=v64[:, c, :], start=True, stop=True),
                          reads=["ks64", "v64"], writes=[("bU", dt)])

                def om(e, c=c, h=h, cs=cs, ob=ob):
                    ins = None
                    for vt in range(4):
                        o_ap = ps[:, ob, vt * 128 + h * 64: vt * 128 + h * 64 + 64]
                        e.matmul(o_ap, lhsT=v64[:, c, vt * 128:(vt + 1) * 128], rhs=sc[:, c, :], start=True, stop=False)
                        for dt in range(2):
                            ins = e.matmul(o_ap, lhsT=Sbf[:, dt, vt * 128:(vt + 1) * 128], rhs=qi[:, dt, cs],
                                           start=False, stop=(dt == 1))
                    return ins
                P.add("pe", om, reads=["v64", "sc", "Sbf0", "Sbf1", ("qi", 0), ("qi", 1)], writes=[okey])
                for dt in range(2):
                    P.add("dve", lambda e, c=c, dt=dt: e.scalar_tensor_tensor(out=S[:, dt, :], in0=S[:, dt, :],
                                                                              scalar=dec[:, dt, c:c + 1], in1=ps[:, B_U0 + dt, :],
                                                                              op0=ALU.mult, op1=ALU.add),
                          reads=[f"S{dt}", ("dec", dt), ("bU", dt)], writes=[f"S{dt}"])
                    P.add("act", lambda e, dt=dt: e.activation(out=Sbf[:, dt, :], in_=S[:, dt, :], func=AF.Copy),
                          reads=[f"S{dt}"], writes=[f"Sbf{dt}"])
            o3 = ps[:, ob, :].rearrange("p (v t) -> p v t", t=128)
            P.add("act", lambda e, o3=o3: e.activation(out=sq[:, :, :], in_=o3, func=AF.Square),
                  reads=[okey], writes=["sq"])

            def stm(e):
                ins = None
                for vt in range(4):
                    ins = e.matmul(ps[:, B_ST, 0:128], lhsT=ones[:, :], rhs=sq[:, vt, :], start=(vt == 0), stop=(vt == 3))
                return ins
            P.add("pe", stm, reads=["sq", "ones"], writes=["bST"])
            P.add("act", lambda e: e.activation(out=rs[:, :], in_=ps[:, B_ST, 0:128], func=AF.Sqrt, scale=1.0 / DV, bias=EPS),
                  reads=["bST"], writes=["rs"])
            P.add("dve", lambda e: e.reciprocal(out=rs[:, :], in_=rs[:, :]), reads=["rs"], writes=["rs"])
            P.add("dve", lambda e, o3=o3: e.tensor_tensor(out=tmp[:, :, :], in0=o3,
                                                          in1=rs[:, :].unsqueeze(1).to_broadcast([128, 4, 128]), op=ALU.mult),
                  reads=[okey, "rs"], writes=["tmp"])
            bs = slice(blk * 128, (blk + 1) * 128)
            P.add("pool", lambda e, bs=bs: e.tensor_tensor(out=gs[:, :, bs], in0=tmp[:, :, :], in1=gs[:, :, bs], op=ALU.mult),
                  reads=["tmp", "gs"], writes=["gs"])
        P.dma("sp", "out_g", out=og_d[:, tok0:tok0 + GT].rearrange("(vt p) t -> p vt t", p=128), in_=gs[:, :, :],
              reads=["gs"], writes=[("ogout", g)])
        fin.append(("ogout", g))
    P.add("sp", None, reads=fin)
    P.build()
    return nc, P


L = 4224
NB = 33
HD = 128
NH = 4
EPS = 1e-6
PADK = 112


def build_l4():
    nc = bass.Bass("TRN2", target_bir_lowering=False)
    qT_d = nc.dram_tensor("qT", [NH * HD, L], BF16, kind="ExternalInput")
    kT_d = nc.dram_tensor("kT", [NH * HD, L], BF16, kind="ExternalInput")
    v_d = nc.dram_tensor("v", [L, NH * HD], BF16, kind="ExternalInput")
    gc_d = nc.dram_tensor("gcols", [128, 2], F32, kind="ExternalInput")
    o_d = nc.dram_tensor("o", [L, NH * HD], BF16, kind="ExternalOutput")

    P = Prog(nc)
    ps = nc.alloc_psum_tensor("ps", [128, 8, 512], F32)
    BA, BB, BO, BC = (0, 1), (2, 3), (4, 5), (6, 7)
    sb = lambda name, shape, dt: nc.alloc_sbuf_tensor(name + "_sb", shape, dt)
    onesb = sb("onesb", [128, 128], BF16)
    onec = sb("onec", [128, 1], BF16)
    ntri = sb("ntri", [128, 128], BF16)
    mdiag = sb("mdiag", [128, 128], BF16)
    kval = sb("kval", [128, 1], F32)
    gc = sb("gc", [128, 2], F32)
    qr = sb("qr", [128, L], BF16)
    kr = sb("kr", [128, L], BF16)
    qn = sb("qn", [128, L], BF16)
    kn = sb("kn", [128, L], BF16)
    v_all = sb("v_all", [128, NB, NH * HD], BF16)
    o_all = sb("o_all", [128, NB, NH * HD], BF16)
    sqb = [sb(f"sqb{i}", [128, 512], BF16) for i in range(2)]
    rst = [sb(f"rst{i}", [128, 512], F32) for i in range(2)]
    ebuf = [sb(f"e{i}", [128, 512], F32) for i in range(2)]
    spb = [sb(f"sp{i}", [128, 512], BF16) for i in range(2)]
    att = [sb(f"att{i}", [128, 512], BF16) for i in range(2)]
    dcy = [sb(f"dcy{i}", [128, 4], F32) for i in range(2)]
    acc = sb("acc", [128, 4, 128], F32)

    P.add("pool", lambda e: e.memset(onesb[:, :], 1.0), writes=["onesb"])
    P.add("pool", lambda e: e.memset(onec[:, :], 1.0), writes=["onec"])
    P.add("pool", lambda e: e.memset(ntri[:, :], -1.0), writes=["ntri"])
    P.add("pool", lambda e: e.affine_select(out=ntri[:, :], in_=ntri[:, :], pattern=[[-1, 128]], compare_op=ALU.is_ge,
                                            fill=0.0, base=0, channel_multiplier=1), reads=["ntri"], writes=["ntri"])
    P.add("pool", lambda e: e.memset(mdiag[:, :], 1.0), writes=["mdiag"])
    P.add("pool", lambda e: e.affine_select(out=mdiag[:, :], in_=mdiag[:, :], pattern=[[1, 128]], compare_op=ALU.is_gt,
                                            fill=0.0, base=0, channel_multiplier=-1), reads=["mdiag"], writes=["mdiag"])
    P.add("pool", lambda e: e.memset(kval[:, :], 1.0), writes=["kval"])
    P.add("pool", lambda e: e.affine_select(out=kval[:, :], in_=kval[:, :], pattern=[[0, 1]], compare_op=ALU.is_ge,
                                            fill=0.0, base=-PADK, channel_multiplier=1), reads=["kval"], writes=["kval"])
    P.dma("sp", "c0", out=gc[:, :], in_=gc_d[:, :], writes=["gc"])
    P.dma("sp", "vin", out=v_all[:, :, :], in_=v_d[:, :].rearrange("(b s) d -> s b d", s=128), writes=["v_all"])

    par = [0]
    npar = [0]
    for h in range(NH):
        P.dma("sp", "qin", out=qr[:, :], in_=qT_d[h * HD:(h + 1) * HD, :], writes=["qr"])
        P.dma("sp", "kin", out=kr[:, :], in_=kT_d[h * HD:(h + 1) * HD, :], writes=["kr"])
        for (src, skey, dst, dkey, gi, sc_, bi_) in ((qr, "qr", qn, "qn", 0, 1.0, HD * EPS), (kr, "kr", kn, "kn", 1, 1.0 / HD, EPS)):
            for o0 in range(0, L, 512):
                w = min(512, L - o0)
                p = npar[0]
                npar[0] ^= 1
                P.add("dve", lambda e, src=src, o0=o0, w=w, p=p: e.tensor_tensor(out=sqb[p][:, 0:w], in0=src[:, o0:o0 + w],
                                                                                 in1=src[:, o0:o0 + w], op=ALU.mult),
                      reads=[skey], writes=[("sqb", p)])
                P.add("pe", lambda e, w=w, p=p: e.matmul(ps[:, BA[p], 0:w], lhsT=onesb[:, :], rhs=sqb[p][:, 0:w], start=True, stop=True),
                      reads=[("sqb", p), "onesb"], writes=[("bA", p)])
                P.add("act", lambda e, w=w, p=p, sc_=sc_, bi_=bi_: e.activation(out=rst[p][:, 0:w], in_=ps[:, BA[p], 0:w], func=AF.Sqrt,
                                                                                scale=sc_, bias=bi_),
                      reads=[("bA", p)], writes=[("rst", p)])
                P.add("dve", lambda e, w=w, p=p: e.reciprocal(out=rst[p][:, 0:w], in_=rst[p][:, 0:w]),
                      reads=[("rst", p)], writes=[("rst", p)])
                P.add("dve", lambda e, src=src, dst=dst, o0=o0, w=w, p=p, gi=gi: e.scalar_tensor_tensor(
                    out=dst[:, o0:o0 + w], in0=src[:, o0:o0 + w], scalar=gc[:, gi:gi + 1], in1=rst[p][:, 0:w],
                    op0=ALU.mult, op1=ALU.mult),
                      reads=[skey, "gc", ("rst", p)], writes=[dkey])
        for I in range((NB + 3) // 4):
            iq0 = 4 * I
            iq1 = min(iq0 + 4, NB)
            nq = iq1 - iq0
            P.add("pool", lambda e, nq=nq: e.memset(acc[:, 0:nq, :], 0.0), writes=["acc"])
            for j in range(iq1):
                i0 = max(iq0, j)
                ncols = (iq1 - i0) * 128
                t0 = i0 * 128
                p = par[0]
                par[0] ^= 1
                kj = kn[:, j * 128:(j + 1) * 128]
                qcols = qn[:, t0:t0 + ncols]
                diag = (j >= iq0)
                P.add("pe", lambda e, kj=kj, qcols=qcols, ncols=ncols, p=p: e.matmul(ps[:, BA[p], 0:ncols], lhsT=kj, rhs=qcols,
                                                                                    start=True, stop=True),
                      reads=["kn", "qn"], writes=[("bA", p)])
                P.add("act", lambda e, ncols=ncols, p=p: e.activation(out=ebuf[p][:, 0:ncols], in_=ps[:, BA[p], 0:ncols], func=AF.Exp),
                      reads=[("bA", p)], writes=[("e", p)])
                P.add("act", lambda e, ncols=ncols, p=p: e.activation(out=spb[p][:, 0:ncols], in_=ebuf[p][:, 0:ncols], func=AF.Ln,
                                                                      scale=1.0, bias=1.0),
                      reads=[("e", p)], writes=[("sp", p)])
                if diag:
                    P.add("pool", lambda e, p=p: e.tensor_tensor(out=spb[p][:, 0:128], in0=spb[p][:, 0:128], in1=mdiag[:, :], op=ALU.mult),
                          reads=[("sp", p), "mdiag"], writes=[("sp", p)])
                if j == 0:
                    P.add("pool", lambda e, ncols=ncols, p=p: e.tensor_scalar(out=spb[p][:, 0:ncols], in0=spb[p][:, 0:ncols],
                                                                              scalar1=kval[:, 0:1], scalar2=1.0, op0=ALU.mult, op1=ALU.mult),
                          reads=[("sp", p), "kval"], writes=[("sp", p)])

                def mmB(e, kj=kj, qcols=qcols, ncols=ncols, p=p, nqb=ncols // 128):
                    e.matmul(ps[:, BB[p], 0:ncols], lhsT=kj, rhs=qcols, start=True, stop=False)
                    e.matmul(ps[:, BB[p], 0:ncols], lhsT=ntri[:, :], rhs=spb[p][:, 0:ncols], start=False, stop=True)
                    ins = None
                    for il in range(nqb):
                        ins = e.matmul(ps[:, BC[p], il:il + 1], lhsT=spb[p][:, il * 128:(il + 1) * 128], rhs=onec[:, :],
                                       start=True, stop=True)
                    return ins
                P.add("pe", mmB, reads=["kn", "qn", ("sp", p), "ntri", "onec"], writes=[("bB", p), ("bC", p)])
                P.add("act", lambda e, ncols=ncols, p=p: e.activation(out=att[p][:, 0:ncols], in_=ps[:, BB[p], 0:ncols], func=AF.Exp),
                      reads=[("bB", p)], writes=[("att", p)])
                nqb = ncols // 128
                P.add("act", lambda e, p=p, nqb=nqb: e.activation(out=dcy[p][:, 0:nqb], in_=ps[:, BC[p], 0:nqb], func=AF.Exp, scale=-1.0),
                      reads=[("bC", p)], writes=[("dcy", p)])
                if diag:
                    P.add("pool", lambda e, p=p: e.tensor_tensor(out=att[p][:, 0:128], in0=att[p][:, 0:128], in1=mdiag[:, :], op=ALU.mult),
                          reads=[("att", p), "mdiag"], writes=[("att", p)])
                if j == 0:
                    P.add("pool", lambda e, ncols=ncols, p=p: e.tensor_scalar(out=att[p][:, 0:ncols], in0=att[p][:, 0:ncols],
                                                                              scalar1=kval[:, 0:1], scalar2=1.0, op0=ALU.mult, op1=ALU.mult),
                          reads=[("att", p), "kval"], writes=[("att", p)])

                def mmO(e, p=p, nqb=nqb, j=j, h=h):
                    ins = None
                    for il in range(nqb):
                        ins = e.matmul(ps[:, BO[p], il * 128:(il + 1) * 128], lhsT=att[p][:, il * 128:(il + 1) * 128],
                                       rhs=v_all[:, j, h * HD:(h + 1) * HD], start=True, stop=True)
                    return ins
                P.add("pe", mmO, reads=[("att", p), "v_all"], writes=[("bO", p)])
                for il in range(nqb):
                    ia = (i0 - iq0) + il
                    P.add("dve", lambda e, p=p, il=il, ia=ia: e.scalar_tensor_tensor(
                        out=acc[:, ia, :], in0=acc[:, ia, :], scalar=dcy[p][:, il:il + 1], in1=ps[:, BO[p], il * 128:(il + 1) * 128],
                        op0=ALU.mult, op1=ALU.add),
                          reads=["acc", ("dcy", p), ("bO", p)], writes=["acc"])
            P.add("act", lambda e, iq0=iq0, iq1=iq1, nq=nq, h=h: e.activation(out=o_all[:, iq0:iq1, h * HD:(h + 1) * HD],
                                                                              in_=acc[:, 0:nq, :], func=AF.Copy),
                  reads=["acc"], writes=["o_all"])
    P.dma("sp", "oout", out=o_d[:, :].rearrange("(b t) d -> t b d", t=128), in_=o_all[:, :, :], reads=["o_all"], writes=["oout"])
    P.add("sp", None, reads=["oout"])
    P.build()
    return nc, P


_PROGS = {}


def _prog(name, builder, *args):
    if name not in _PROGS:
        _PROGS[name] = builder(*args)[0]
    return _PROGS[name]


def _cols(vec):
    v = np.asarray(vec, dtype=np.float32)
    return np.ascontiguousarray(v.reshape(-1, 128).T)


def _run(nc, in_maps):
    res = run_bass_kernel_spmd(nc, in_maps, core_ids=list(range(8)))
    return res.results


def kernel(x, meta_tokens, g_norm_a, w_in_a, w_gate_up_a, b_gate_a, g_onorm_a, w_out_a,
           g_kv_norm, w_kv, g_k, g_norm_b, w_q_b, g_q_b, w_o_b,
           g_ffn_norm, w_ffn_gate, w_ffn_up, w_ffn_down):
    f32 = np.float32
    x = np.asarray(x, f32)
    B = x.shape[0]
    meta = np.asarray(meta_tokens, f32)
    h0 = np.concatenate([np.zeros((B, 112, D), f32), np.broadcast_to(meta[None], (B, 16, D)), x], axis=1)
    Ltot = h0.shape[1]
    h0f = h0.reshape(B * Ltot, D)
    C = np.ascontiguousarray
    hT0 = [C(h0f[c * T:(c + 1) * T].T) for c in range(8)]

    w_in = C(np.asarray(w_in_a, f32)[0])
    gcol = _cols(np.asarray(g_norm_a)[0])
    r1 = _run(_prog("l1", build_l1), [{"hT": hT0[c], "w_in": w_in, "gcol": gcol} for c in range(8)])
    PT = [np.concatenate([np.asarray(r1[4 * b + s]["projT"]) for s in range(4)], axis=1) for b in range(B)]
    AT = [C(np.concatenate([np.asarray(r1[4 * b + s]["aT"]) for s in range(4)], axis=1)) for b in range(B)]

    wgu_full = np.asarray(w_gate_up_a, f32)[0]
    bg = np.asarray(b_gate_a, f32)[0]
    gon = np.asarray(g_onorm_a, f32)[0]
    m2 = []
    for c in range(8):
        b, hd = divmod(c, 4)
        cols = np.zeros((128, 8), f32)
        cols[:, 0:2] = bg[hd * DK:(hd + 1) * DK].reshape(2, 128).T
        cols[:, 2:6] = gon.reshape(4, 128).T
        m2.append({"qT": C(PT[b][hd * DK:(hd + 1) * DK]), "kT": C(PT[b][1024 + hd * DK:1024 + (hd + 1) * DK]),
                   "v": C(PT[b][2048 + hd * DV:2048 + (hd + 1) * DV].T), "gT": C(PT[b][4096 + hd * DV:4096 + (hd + 1) * DV]),
                   "aT": AT[b], "wgu": C(wgu_full[:, hd * DK:(hd + 1) * DK]), "cols": cols})
    r2 = _run(_prog("l2", build_l2), m2)
    OG = [np.concatenate([np.asarray(r2[4 * b + hd]["ogT"]) for hd in range(4)], axis=0) for b in range(B)]

    gffn = np.asarray(g_ffn_norm, f32)
    gc3 = C(np.concatenate([_cols(gffn[0]), _cols(np.asarray(g_kv_norm)), _cols(np.asarray(g_norm_b)[0])], axis=1))
    wmix = C(np.asarray(w_out_a, f32)[0])
    wg0 = C(np.asarray(w_ffn_gate, f32)[0]); wu0 = C(np.asarray(w_ffn_up, f32)[0]); wd0 = C(np.asarray(w_ffn_down, f32)[0])
    wkv = C(np.asarray(w_kv, f32)); wq = C(np.asarray(w_q_b, f32)[0])
    m3 = []
    for c in range(8):
        b, s = divmod(c, 4)
        m3.append({"hT": hT0[c], "oT": C(OG[b][:, s * T:(s + 1) * T]), "w_mix": wmix, "w_gate": wg0, "w_up": wu0,
                   "w_down": wd0, "gcols": gc3, "w_kv": wkv, "w_q": wq})
    r3 = _run(_prog("l3", build_l35, "L3"), m3)
    hT2 = [C(np.asarray(r3[c]["hout"])) for c in range(8)]
    KV = [np.concatenate([np.asarray(r3[4 * b + s]["kvT"]) for s in range(4)], axis=1) for b in range(B)]
    QQ = [np.concatenate([np.asarray(r3[4 * b + s]["qT"]) for s in range(4)], axis=1) for b in range(B)]

    gc4 = C(np.stack([np.asarray(g_q_b, f32)[0], np.asarray(g_k, f32)], axis=1))
    m4 = []
    for c in range(8):
        b, hg = divmod(c, 4)
        m4.append({"qT": C(QQ[b][hg * 512:(hg + 1) * 512]), "kT": C(KV[b][hg * 512:(hg + 1) * 512]),
                   "v": C(KV[b][2048 + hg * 512:2048 + (hg + 1) * 512].T), "gcols": gc4})
    r4 = _run(_prog("l4", build_l4), m4)
    OO = [np.concatenate([np.asarray(r4[4 * b + hg]["o"]) for hg in range(4)], axis=1) for b in range(B)]

    gc5 = C(np.concatenate([_cols(gffn[1]), _cols(gffn[1]), _cols(gffn[1])], axis=1))
    wo = C(np.asarray(w_o_b, f32)[0])
    wg1 = C(np.asarray(w_ffn_gate, f32)[1]); wu1 = C(np.asarray(w_ffn_up, f32)[1]); wd1 = C(np.asarray(w_ffn_down, f32)[1])
    m5 = []
    for c in range(8):
        b, s = divmod(c, 4)
        m5.append({"hT": hT2[c], "oT": C(OO[b][s * T:(s + 1) * T].T), "w_mix": wo, "w_gate": wg1, "w_up": wu1,
                   "w_down": wd1, "gcols": gc5})
    r5 = _run(_prog("l5", build_l35, "L5"), m5)
    hfin = np.concatenate([np.asarray(r5[c]["hout"]).T for c in range(8)], axis=0).reshape(B, Ltot, D)
    return np.ascontiguousarray(hfin[:, 128:, :].astype(np.float32))
```

```python
import numpy as np
import ml_dtypes
from concourse.bass_utils import run_bass_kernel_spmd
import numpy as np
import concourse.bass as bass
import concourse.mybir as mybir

F32 = mybir.dt.float32
BF16 = mybir.dt.bfloat16
AF = mybir.ActivationFunctionType
ALU = mybir.AluOpType
AX = mybir.AxisListType

ENGS = ("pe", "act", "dve", "pool", "sp")


class Op:
    __slots__ = ("idx", "eng", "fn", "reads", "writes", "dma", "deps", "signal", "count",
                 "pre_wait", "group", "waits", "inc", "barrier")

    def __init__(self, idx, eng, fn, reads, writes, dma):
        self.idx = idx
        self.eng = eng
        self.fn = fn
        self.reads = tuple(reads)
        self.writes = tuple(writes)
        self.dma = dma
        self.deps = []
        self.signal = False
        self.count = None
        self.pre_wait = None
        self.group = None
        self.waits = []
        self.inc = 16
        self.barrier = False


class Prog:
    def __init__(self, nc):
        self.nc = nc
        self.ops = []
        self.tri = 0
        self.cache = {}

    def add(self, eng, fn, reads=(), writes=(), dma=None):
        op = Op(len(self.ops), eng, fn, reads, writes, dma)
        self.ops.append(op)
        return op

    def dma(self, eng, slot, out, in_, reads=(), writes=()):
        def fn(e, out=out, in_=in_):
            return e.dma_start(out=out, in_=in_)
        return self.add(eng, fn, reads, writes, dma=slot)

    def coll(self, slot, fn, reads=(), writes=()):
        op = self.add("pool", fn, reads, writes, dma=slot)
        op.inc = 1
        return op

    def barrier(self):
        op = self.add("sp", lambda e: e.nop(), (), ())
        op.barrier = True
        return op

    def next_tri(self):
        t = self.tri
        self.tri ^= 1
        return t

    def build(self):
        nc = self.nc
        ops = self.ops
        last_writer = {}
        readers = {}
        dcount = {}
        dgroup = {}
        dclosed = {}
        gend = {}
        cur_barrier = None
        for op in ops:
            deps = set()
            if op.barrier:
                for k, w in last_writer.items():
                    deps.add(w)
                for k, rs in readers.items():
                    deps.update(rs)
                last_writer = {}
                readers = {}
            elif cur_barrier is not None:
                deps.add(cur_barrier)
            for k in op.reads:
                if k in last_writer:
                    deps.add(last_writer[k])
            for k in op.writes:
                if k in last_writer:
                    deps.add(last_writer[k])
                for r in readers.get(k, ()):
                    deps.add(r)
            deps.discard(op.idx)
            op.deps = sorted(deps)
            for di in op.deps:
                p = ops[di]
                if p.dma is not None:
                    s = p.dma
                    if p.group == dgroup[s]:
                        dclosed[s] = True
                        val = dcount[s]
                        gend[(s, p.group)] = val
                    else:
                        val = gend[(s, p.group)]
                    op.waits.append((("dma", s), val))
                else:
                    if p.eng == "pe" and op.eng == "pe" and op.dma is None:
                        continue
                    p.signal = True
                    op.waits.append((("eng", p.eng), di))
            if op.dma is not None:
                s = op.dma
                if s not in dcount:
                    dcount[s] = 0
                    dgroup[s] = 0
                    dclosed[s] = False
                if dclosed[s]:
                    gend[(s, dgroup[s])] = dcount[s]
                    op.pre_wait = (("dma", s), dcount[s])
                    dgroup[s] += 1
                    dclosed[s] = False
                dcount[s] += op.inc
                op.group = dgroup[s]
            if op.barrier:
                cur_barrier = op.idx
            for k in op.reads:
                readers.setdefault(k, []).append(op.idx)
            for k in op.writes:
                last_writer[k] = op.idx
                readers[k] = []
        cnt = {e: 0 for e in ENGS}
        for op in ops:
            if op.dma is None and op.signal:
                cnt[op.eng] += 1
                op.count = cnt[op.eng]
        self.max_counts = dict(cnt)
        sems = {}
        for e in ENGS:
            if cnt[e] > 0:
                sems[("eng", e)] = nc.alloc_semaphore("s_" + e)
        for s in dcount:
            sems[("dma", s)] = nc.alloc_semaphore("d_" + str(s))
        self.sems = sems
        per_eng = {e: [op for op in ops if op.eng == e] for e in ENGS}

        def emit(ename, eng):
            seen = {}
            for op in per_eng[ename]:
                ws = []
                if op.pre_wait is not None:
                    ws.append(op.pre_wait)
                for (sk, v) in op.waits:
                    if sk[0] == "eng":
                        v = ops[v].count
                    ws.append((sk, v))
                for (sk, v) in ws:
                    if v <= seen.get(sk, 0):
                        continue
                    seen[sk] = v
                    eng.wait_ge(sems[sk], v)
                if op.fn is None:
                    continue
                ins = op.fn(eng)
                if op.dma is not None:
                    ins.then_inc(sems[("dma", op.dma)], op.inc)
                elif op.signal:
                    ins.then_inc(sems[("eng", ename)], 1)

        with nc.Block() as block:
            @block.tensor
            def _(e):
                emit("pe", e)

            @block.scalar
            def _(e):
                emit("act", e)

            @block.vector
            def _(e):
                emit("dve", e)

            @block.gpsimd
            def _(e):
                emit("pool", e)

            @block.sync
            def _(e):
                emit("sp", e)


class Arena:
    def __init__(self, nc, nbytes):
        self.n16 = nbytes // 2
        self.t = nc.alloc_sbuf_tensor("arena", [128, self.n16], BF16)
        self.off = 0

    def mark(self):
        return self.off

    def reset(self, off):
        self.off = off

    def alloc(self, shape, dtype):
        n = 1
        for s in shape[1:]:
            n *= s
        e16 = n * (2 if dtype == F32 else 1)
        e16 = (e16 + 15) // 16 * 16
        assert self.off + e16 <= self.n16, ("arena overflow", self.off, e16, self.n16)
        ap = self.t[0:shape[0], self.off:self.off + (n * (2 if dtype == F32 else 1))]
        self.off += e16
        if dtype == F32:
            ap = ap.bitcast(F32)
        if len(shape) == 3:
            ap = ap.rearrange("p (a b) -> p a b", b=shape[2])
        return ap


L = 4224
NB = 33
GB = 11
NG = 3
GT = GB * 128
GC = GB * 2
DK = 256
DV = 512
EPS = 1e-6


def gla_phase(nc, P, A, ps, qT_d, kT_d, v_d, gT_d, aT_d, wgu_d, cols_d, og_d, on_store=None):
    pst = ps[:, 7, :].bitcast(BF16)
    B_Z, B_UP, B_U0, B_U1, B_O0, B_O1, B_ST = 0, 1, 2, 3, 4, 5, 6
    sb = lambda name, shape, dt: A.alloc(shape, dt)
    ones = sb("ones", [128, 128], BF16)
    ident = sb("ident", [128, 128], BF16)
    mlo = sb("mlo", [64, 64], F32)
    mup = sb("mup", [64, 64], F32)
    cols = sb("cols", [128, 8], F32)
    wgu = sb("wgu", [16, DK], F32)
    aT = sb("aT", [16, GT], F32)
    qT = sb("qT", [128, 2, GT], BF16)
    kT = sb("kT", [128, 2, GT], BF16)
    v64 = sb("v64", [64, GC, DV], BF16)
    gs = sb("gs", [128, 4, GT], BF16)
    et = sb("et", [128, GT], F32)
    cc = sb("cc", [128, GT], F32)
    dd = sb("dd", [128, GT], F32)
    ep = sb("ep", [128, GT], F32)
    em = sb("em", [128, GT], F32)
    rmask = sb("rmask", [128, GT], F32)
    al = sb("al", [128, GC], F32)
    be = sb("be", [128, GC], F32)
    dec = sb("dec", [128, 2, GC], F32)
    tsm = sb("tsm", [128, GC], F32)
    qa = sb("qa", [128, 2, GT], BF16)
    qb = sb("qb", [128, 2, GT], BF16)
    ka = sb("ka", [128, 2, GT], BF16)
    kb = sb("kb", [128, 2, GT], BF16)
    qi = sb("qi", [128, 2, GT], BF16)
    ksT = sb("ksT", [128, 2, GT], BF16)
    ks64 = sb("ks64", [64, GC, DK], BF16)
    sc = sb("sc", [64, GC, 64], BF16)
    t1 = sb("t1", [64, 8, 64], F32)
    t2 = sb("t2", [64, 8, 64], F32)
    S = sb("S", [128, 2, DV], F32)
    Sbf = sb("Sbf", [128, 2, DV], BF16)
    sq = sb("sq", [128, 4, 128], BF16)
    rs = sb("rs", [128, 128], F32)
    tmp = sb("tmp", [128, 4, 128], F32)

    P.add("pool", lambda e: e.memset(ones[:, :], 1.0), writes=["ones"])
    P.add("pool", lambda e: e.memset(ident[:, :], 1.0), writes=["ident"])
    P.add("pool", lambda e: e.affine_select(out=ident[:, :], in_=ident[:, :], pattern=[[-1, 128]],
                                            compare_op=ALU.is_equal, fill=0.0, base=0, channel_multiplier=1),
          reads=["ident"], writes=["ident"])
    P.add("pool", lambda e: e.memset(mlo[:, :], 1.0), writes=["mlo"])
    P.add("pool", lambda e: e.affine_select(out=mlo[:, :], in_=mlo[:, :], pattern=[[1, 64]],
                                            compare_op=ALU.is_ge, fill=0.0, base=0, channel_multiplier=-1),
          reads=["mlo"], writes=["mlo"])
    P.add("pool", lambda e: e.memset(mup[:, :], 1.0), writes=["mup"])
    P.add("pool", lambda e: e.affine_select(out=mup[:, :], in_=mup[:, :], pattern=[[-1, 64]],
                                            compare_op=ALU.is_gt, fill=0.0, base=0, channel_multiplier=1),
          reads=["mup"], writes=["mup"])
    P.add("pool", lambda e: e.memset(rmask[:, :], 1.0), writes=["rmask"])
    P.add("pool", lambda e: e.memset(rmask[:, :].rearrange("p (c t) -> p c t", t=64)[:, :, 0:1], 0.0),
          reads=["rmask"], writes=["rmask"])
    P.add("pool", lambda e: e.memset(S[:, :, :], 0.0), writes=["S0", "S1"])
    P.add("pool", lambda e: e.memset(Sbf[:, :, :], 0.0), writes=["Sbf0", "Sbf1"])
    P.dma("sp", "c0", out=cols[:, :], in_=cols_d[:, :], writes=["cols"])
    P.dma("sp", "c0", out=wgu[:, :], in_=wgu_d[:, :], writes=["wgu"])
    negb = sb("negb", [128, 2], F32)
    P.add("pool", lambda e: e.tensor_scalar(out=negb[:, :], in0=cols[:, 0:2], scalar1=-1.0, scalar2=1.0, op0=ALU.mult, op1=ALU.mult),
          reads=["cols"], writes=["negb"])

    c3 = lambda ap: ap.rearrange("p (c t) -> p c t", t=64)
    fin = []
    obank = [0]

    for g in range(NG):
        tok0 = g * GT
        P.dma("sp", "in_a", out=aT[:, :], in_=aT_d[:, tok0:tok0 + GT], writes=["aT"])
        P.dma("sp", "in_q", out=qT[:, :, :], in_=qT_d[:, tok0:tok0 + GT].rearrange("(dt p) t -> p dt t", p=128),
              writes=["qT"])
        P.dma("sp", "in_k", out=kT[:, :, :], in_=kT_d[:, tok0:tok0 + GT].rearrange("(dt p) t -> p dt t", p=128),
              writes=["kT"])
        P.dma("sp", "in_v", out=v64[:, :, :], in_=v_d[tok0:tok0 + GT, :].rearrange("(c s) v -> s c v", s=64),
              writes=["v64"])
        P.dma("sp", "in_g", out=gs[:, :, :], in_=gT_d[:, tok0:tok0 + GT].rearrange("(vt p) t -> p vt t", p=128),
              writes=["gs"])
        for vt in range(4):
            P.add("act", lambda e, vt=vt: e.activation(out=gs[:, vt, :], in_=gs[:, vt, :], func=AF.Silu),
                  reads=["gs"], writes=["gs"])
            P.add("pool", lambda e, vt=vt: e.tensor_scalar(out=gs[:, vt, :], in0=gs[:, vt, :],
                                                           scalar1=cols[:, 2 + vt:3 + vt], scalar2=1.0, op0=ALU.mult, op1=ALU.mult),
                  reads=["gs", "cols"], writes=["gs"])
        for dt in range(2):
            nch = [(i * 512, min(512, GT - i * 512)) for i in range((GT + 511) // 512)]
            for (o0, w) in nch:
                P.add("pe", lambda e, o0=o0, w=w, dt=dt: e.matmul(ps[:, B_Z, 0:w], lhsT=wgu[:, dt * 128:(dt + 1) * 128],
                                                                  rhs=aT[:, o0:o0 + w], start=True, stop=True),
                      reads=["wgu", "aT"], writes=["bZ"])
                P.add("act", lambda e, o0=o0, w=w, dt=dt: e.activation(out=et[:, o0:o0 + w], in_=ps[:, B_Z, 0:w], func=AF.Exp,
                                                                       scale=-1.0, bias=negb[:, dt:dt + 1]),
                      reads=["bZ", "negb"], writes=["et"])
            P.add("act", lambda e: e.activation(out=cc[:, :], in_=et[:, :], func=AF.Ln, scale=1.0, bias=1.0),
                  reads=["et"], writes=["cc"])
            P.add("dve", lambda e: e.tensor_tensor_scan(out=cc[:, :], data0=rmask[:, :], data1=cc[:, :], initial=0.0,
                                                        op0=ALU.mult, op1=ALU.add),
                  reads=["cc", "rmask"], writes=["cc"])
            P.add("dve", lambda e: e.tensor_tensor(out=c3(dd[:, :]), in0=c3(cc[:, :]),
                                                   in1=c3(cc[:, :])[:, :, 31:32].to_broadcast([128, GC, 64]),
                                                   op=ALU.subtract),
                  reads=["cc"], writes=["dd"])
            P.add("act", lambda e: e.activation(out=ep[:, :], in_=dd[:, :], func=AF.Exp, scale=-1.0 / 16),
                  reads=["dd"], writes=["ep"])
            P.add("act", lambda e: e.activation(out=em[:, :], in_=dd[:, :], func=AF.Exp, scale=1.0 / 16),
                  reads=["dd"], writes=["em"])
            P.add("act", lambda e: e.activation(out=al[:, :], in_=c3(cc[:, :])[:, :, 31], func=AF.Exp, scale=-1.0 / 16),
                  reads=["cc"], writes=["al"])
            P.add("act", lambda e, dt=dt: e.activation(out=dec[:, dt, :], in_=c3(cc[:, :])[:, :, 63], func=AF.Exp,
                                                       scale=-1.0 / 16),
                  reads=["cc"], writes=[("dec", dt)])
            P.add("dve", lambda e: e.tensor_tensor(out=tsm[:, :], in0=c3(cc[:, :])[:, :, 63], in1=c3(cc[:, :])[:, :, 31],
                                                   op=ALU.subtract),
                  reads=["cc"], writes=["tsm"])
            P.add("act", lambda e: e.activation(out=be[:, :], in_=tsm[:, :], func=AF.Exp, scale=-1.0 / 16),
                  reads=["tsm"], writes=["be"])
            P.add("dve", lambda e, dt=dt: e.scalar_tensor_tensor(out=qa[:, dt, :], in0=qT[:, dt, :], scalar=1.0 / 16,
                                                                 in1=ep[:, :], op0=ALU.mult, op1=ALU.mult),
                  reads=["qT", "ep"], writes=[("qa", dt)])
            P.add("dve", lambda e, dt=dt: e.scalar_tensor_tensor(out=qb[:, dt, :], in0=qT[:, dt, :], scalar=1.0 / 16,
                                                                 in1=em[:, :], op0=ALU.mult, op1=ALU.mult),
                  reads=["qT", "em"], writes=[("qb", dt)])
            P.add("dve", lambda e, dt=dt: e.tensor_tensor(out=ka[:, dt, :], in0=kT[:, dt, :], in1=em[:, :], op=ALU.mult),
                  reads=["kT", "em"], writes=[("ka", dt)])
            P.add("dve", lambda e, dt=dt: e.tensor_tensor(out=kb[:, dt, :], in0=kT[:, dt, :], in1=ep[:, :], op=ALU.mult),
                  reads=["kT", "ep"], writes=[("kb", dt)])
            P.add("dve", lambda e, dt=dt: e.tensor_tensor(out=c3(qi[:, dt, :]), in0=c3(qa[:, dt, :]),
                                                          in1=al[:, :].unsqueeze(2).to_broadcast([128, GC, 64]), op=ALU.mult),
                  reads=[("qa", dt), "al"], writes=[("qi", dt)])
            P.add("dve", lambda e, dt=dt: e.tensor_tensor(out=c3(ksT[:, dt, :]), in0=c3(ka[:, dt, :]),
                                                          in1=be[:, :].unsqueeze(2).to_broadcast([128, GC, 64]), op=ALU.mult),
                  reads=[("ka", dt), "be"], writes=[("ksT", dt)])
        for c0 in range(0, GC, 4):
            n = min(4, GC - c0)

            def tr(e, c0=c0, n=n):
                ins = None
                for j in range(n):
                    for dt in range(2):
                        ins = e.transpose(out=pst[0:64, (j * 2 + dt) * 128:(j * 2 + dt + 1) * 128],
                                          in_=ksT[:, dt, (c0 + j) * 64:(c0 + j + 1) * 64], identity=ident[:, :])
                return ins
            P.add("pe", tr, reads=[("ksT", 0), ("ksT", 1), "ident"], writes=["pst"])
            P.add("act", lambda e, c0=c0, n=n: e.activation(
                out=ks64[:, c0:c0 + n, :], in_=pst[0:64, 0:n * 256].rearrange("p (c d) -> p c d", d=256), func=AF.Copy),
                reads=["pst"], writes=["ks64"])
        for c0 in range(0, GC, 8):
            n = min(8, GC - c0)

            def scm(e, c0=c0, n=n):
                ins = None
                for j in range(n):
                    cs = slice((c0 + j) * 64, (c0 + j + 1) * 64)
                    for dt in range(2):
                        e.matmul(ps[0:64, B_Z, j * 64:(j + 1) * 64], lhsT=ka[:, dt, cs], rhs=qa[:, dt, cs],
                                 start=(dt == 0), stop=(dt == 1))
                    for dt in range(2):
                        ins = e.matmul(ps[0:64, B_UP, j * 64:(j + 1) * 64], lhsT=kb[:, dt, cs], rhs=qb[:, dt, cs],
                                       start=(dt == 0), stop=(dt == 1))
                return ins
            P.add("pe", scm, reads=[("ka", 0), ("ka", 1), ("qa", 0), ("qa", 1), ("kb", 0), ("kb", 1), ("qb", 0), ("qb", 1)],
                  writes=["bZ", "bUP"])
            P.add("dve", lambda e, n=n: e.tensor_tensor(out=t1[:, 0:n, :], in0=c3(ps[0:64, B_Z, 0:n * 64]),
                                                        in1=mlo[:, :].unsqueeze(1).to_broadcast([64, n, 64]), op=ALU.mult),
                  reads=["bZ", "mlo"], writes=["t1"])
            P.add("dve", lambda e, n=n: e.tensor_tensor(out=t2[:, 0:n, :], in0=c3(ps[0:64, B_UP, 0:n * 64]),
                                                        in1=mup[:, :].unsqueeze(1).to_broadcast([64, n, 64]), op=ALU.mult),
                  reads=["bUP", "mup"], writes=["t2"])
            P.add("pool", lambda e, c0=c0, n=n: e.tensor_tensor(out=sc[:, c0:c0 + n, :], in0=t1[:, 0:n, :], in1=t2[:, 0:n, :],
                                                                op=ALU.add),
                  reads=["t1", "t2"], writes=["sc"])
        def ubank(dt, c):
            return (B_U0 + dt) if c % 2 == 0 else (B_Z + dt)

        def ukey(dt, c):
            return ("bU", dt) if c % 2 == 0 else ("bZ" if dt == 0 else "bUP")

        def emit_u(c):
            for dt in range(2):
                P.add("pe", lambda e, c=c, dt=dt: e.matmul(ps[:, ubank(dt, c), :], lhsT=ks64[:, c, dt * 128:(dt + 1) * 128],
                                                           rhs=v64[:, c, :], start=True, stop=True),
                      reads=["ks64", "v64"], writes=[ukey(dt, c)])

        def emit_epilogue(blk, ob, okey):
            o3 = ps[:, ob, :].rearrange("p (v t) -> p v t", t=128)
            P.add("act", lambda e: e.activation(out=sq[:, :, :], in_=o3, func=AF.Square), reads=[okey], writes=["sq"])

            def stm(e):
                ins = None
                for vt in range(4):
                    ins = e.matmul(ps[:, B_ST, 0:128], lhsT=ones[:, :], rhs=sq[:, vt, :], start=(vt == 0), stop=(vt == 3))
                return ins
            P.add("pe", stm, reads=["sq", "ones"], writes=["bST"])
            P.add("act", lambda e: e.activation(out=rs[:, :], in_=ps[:, B_ST, 0:128], func=AF.Sqrt, scale=1.0 / DV, bias=EPS),
                  reads=["bST"], writes=["rs"])
            P.add("dve", lambda e: e.reciprocal(out=rs[:, :], in_=rs[:, :]), reads=["rs"], writes=["rs"])
            P.add("dve", lambda e: e.tensor_tensor(out=tmp[:, :, :], in0=o3, in1=rs[:, :].unsqueeze(1).to_broadcast([128, 4, 128]),
                                                   op=ALU.mult),
                  reads=[okey, "rs"], writes=["tmp"])
            bs = slice(blk * 128, (blk + 1) * 128)
            P.add("pool", lambda e: e.tensor_tensor(out=gs[:, :, bs], in0=tmp[:, :, :], in1=gs[:, :, bs], op=ALU.mult),
                  reads=["tmp", "gs"], writes=["gs"])

        emit_u(0)
        pending = None
        for blk in range(GB):
            ob = B_O0 + obank[0]
            obank[0] ^= 1
            okey = ("bO", ob)
            for h in range(2):
                c = blk * 2 + h
                cs = slice(c * 64, (c + 1) * 64)
                if c + 1 < GC:
                    emit_u(c + 1)

                def om(e, c=c, h=h, cs=cs, ob=ob):
                    ins = None
                    for vt in range(4):
                        o_ap = ps[:, ob, vt * 128 + h * 64: vt * 128 + h * 64 + 64]
                        e.matmul(o_ap, lhsT=v64[:, c, vt * 128:(vt + 1) * 128], rhs=sc[:, c, :], start=True, stop=False)
                        for dt in range(2):
                            ins = e.matmul(o_ap, lhsT=Sbf[:, dt, vt * 128:(vt + 1) * 128], rhs=qi[:, dt, cs],
                                           start=False, stop=(dt == 1))
                    return ins
                P.add("pe", om, reads=["v64", "sc", "Sbf0", "Sbf1", ("qi", 0), ("qi", 1)], writes=[okey])
                for dt in range(2):
                    P.add("dve", lambda e, c=c, dt=dt: e.scalar_tensor_tensor(out=S[:, dt, :], in0=S[:, dt, :],
                                                                              scalar=dec[:, dt, c:c + 1], in1=ps[:, ubank(dt, c), :],
                                                                              op0=ALU.mult, op1=ALU.add),
                          reads=[f"S{dt}", ("dec", dt), ukey(dt, c)], writes=[f"S{dt}"])
                    P.add("act", lambda e, dt=dt: e.activation(out=Sbf[:, dt, :], in_=S[:, dt, :], func=AF.Copy),
                          reads=[f"S{dt}"], writes=[f"Sbf{dt}"])
                if h == 0 and pending is not None:
                    emit_epilogue(*pending)
                    pending = None
            pending = (blk, ob, okey)
        emit_epilogue(*pending)
        gkeys = []
        og_m, og_t = og_d
        for s in range(tok0 // 1056, (tok0 + GT - 1) // 1056 + 1):
            lo = max(tok0, s * 1056)
            hi = min(tok0 + GT, (s + 1) * 1056)
            a, b = lo - s * 1056, hi - s * 1056
            if a < 1024:
                b1 = min(b, 1024)
                P.dma("sp", "out_g", out=og_m[s * DV:(s + 1) * DV, a:b1].rearrange("(vt p) t -> p vt t", p=128),
                      in_=gs[:, :, lo - tok0:lo - tok0 + (b1 - a)], reads=["gs"], writes=[("ogout", g, s, "m")])
                fin.append(("ogout", g, s, "m"))
                gkeys.append(("ogout", g, s, "m"))
            if b > 1024:
                a1 = max(a, 1024)
                P.dma("sp", "out_g", out=og_t[s * DV:(s + 1) * DV, a1 - 1024:b - 1024].rearrange("(vt p) t -> p vt t", p=128),
                      in_=gs[:, :, s * 1056 + a1 - tok0:s * 1056 + b - tok0], reads=["gs"], writes=[("ogout", g, s, "t")])
                fin.append(("ogout", g, s, "t"))
                gkeys.append(("ogout", g, s, "t"))
        if on_store is not None:
            on_store(g, gkeys)
    return fin


L = 4224
NB = 33
HD = 128
NH = 4
EPS = 1e-6
PADK = 112


def sb_phase(nc, P, A, ps, qT_d, kT_d, v_d, gc_d, oT_d):
    pst = ps[:, 7, :].bitcast(BF16)
    BA, BB, BO = (0, 1), (2, 3), (4, 5)
    sb = lambda name, shape, dt: A.alloc(shape, dt)
    onesb = sb("onesb", [128, 128], BF16)
    onec = sb("onec", [128, 1], BF16)
    ntri = sb("ntri", [128, 128], BF16)
    mdiag = sb("mdiag", [128, 128], BF16)
    kval = sb("kval", [128, 1], F32)
    gc = sb("gc", [128, 2], F32)
    qr = sb("qr", [128, L], BF16)
    kr = sb("kr", [128, L], BF16)
    qn = sb("qn", [128, L], BF16)
    kn = sb("kn", [128, L], BF16)
    v_all = sb("v_all", [128, NB, NH * HD], BF16)
    o_h = sb("o_h", [128, NB, HD], BF16)
    oT_sb = sb("oT_sb", [128, NH, L], BF16)
    ident = sb("ident", [128, 128], BF16)
    sqb = [sb(f"sqb{i}", [128, 512], BF16) for i in range(2)]
    rst = [sb(f"rst{i}", [128, 512], F32) for i in range(2)]
    ebuf = [sb(f"e{i}", [128, 512], F32) for i in range(2)]
    spb = [sb(f"sp{i}", [128, 512], BF16) for i in range(2)]
    att = [sb(f"att{i}", [128, 512], BF16) for i in range(2)]
    dcy = [sb(f"dcy{i}", [128, 4], F32) for i in range(2)]
    acc = sb("acc", [128, 4, 128], F32)

    P.add("pool", lambda e: e.memset(onesb[:, :], 1.0), writes=["onesb"])
    P.add("pool", lambda e: e.memset(onec[:, :], 1.0), writes=["onec"])
    P.add("pool", lambda e: e.memset(ntri[:, :], -1.0), writes=["ntri"])
    P.add("pool", lambda e: e.affine_select(out=ntri[:, :], in_=ntri[:, :], pattern=[[-1, 128]], compare_op=ALU.is_ge,
                                            fill=0.0, base=0, channel_multiplier=1), reads=["ntri"], writes=["ntri"])
    P.add("pool", lambda e: e.memset(mdiag[:, :], 1.0), writes=["mdiag"])
    P.add("pool", lambda e: e.affine_select(out=mdiag[:, :], in_=mdiag[:, :], pattern=[[1, 128]], compare_op=ALU.is_gt,
                                            fill=0.0, base=0, channel_multiplier=-1), reads=["mdiag"], writes=["mdiag"])
    P.add("pool", lambda e: e.memset(kval[:, :], 1.0), writes=["kval"])
    P.add("pool", lambda e: e.affine_select(out=kval[:, :], in_=kval[:, :], pattern=[[0, 1]], compare_op=ALU.is_ge,
                                            fill=0.0, base=-PADK, channel_multiplier=1), reads=["kval"], writes=["kval"])
    P.add("pool", lambda e: e.memset(ident[:, :], 1.0), writes=["ident"])
    P.add("pool", lambda e: e.affine_select(out=ident[:, :], in_=ident[:, :], pattern=[[-1, 128]],
                                            compare_op=ALU.is_equal, fill=0.0, base=0, channel_multiplier=1),
          reads=["ident"], writes=["ident"])
    P.dma("sp", "c0", out=gc[:, :], in_=gc_d[:, :], writes=["gc"])
    P.dma("sp", "vin", out=v_all[:, :, :], in_=v_d[:, :].rearrange("(b s) d -> s b d", s=128), writes=["v_all"])

    par = [0]
    npar = [0]
    for h in range(NH):
        P.dma("sp", "qin", out=qr[:, :], in_=qT_d[h * HD:(h + 1) * HD, :], writes=["qr"])
        P.dma("sp", "kin", out=kr[:, :], in_=kT_d[h * HD:(h + 1) * HD, :], writes=["kr"])
        for (src, skey, dst, dkey, gi, sc_, bi_) in ((qr, "qr", qn, "qn", 0, 1.0, HD * EPS), (kr, "kr", kn, "kn", 1, 1.0 / HD, EPS)):
            for o0 in range(0, L, 512):
                w = min(512, L - o0)
                p = npar[0]
                npar[0] ^= 1
                P.add("dve", lambda e, src=src, o0=o0, w=w, p=p: e.tensor_tensor(out=sqb[p][:, 0:w], in0=src[:, o0:o0 + w],
                                                                                 in1=src[:, o0:o0 + w], op=ALU.mult),
                      reads=[skey], writes=[("sqb", p)])
                P.add("pe", lambda e, w=w, p=p: e.matmul(ps[:, BA[p], 0:w], lhsT=onesb[:, :], rhs=sqb[p][:, 0:w], start=True, stop=True),
                      reads=[("sqb", p), "onesb"], writes=[("bA", p)])
                P.add("act", lambda e, w=w, p=p, sc_=sc_, bi_=bi_: e.activation(out=rst[p][:, 0:w], in_=ps[:, BA[p], 0:w], func=AF.Sqrt,
                                                                                scale=sc_, bias=bi_),
                      reads=[("bA", p)], writes=[("rst", p)])
                P.add("dve", lambda e, w=w, p=p: e.reciprocal(out=rst[p][:, 0:w], in_=rst[p][:, 0:w]),
                      reads=[("rst", p)], writes=[("rst", p)])
                P.add("dve", lambda e, src=src, dst=dst, o0=o0, w=w, p=p, gi=gi: e.scalar_tensor_tensor(
                    out=dst[:, o0:o0 + w], in0=src[:, o0:o0 + w], scalar=gc[:, gi:gi + 1], in1=rst[p][:, 0:w],
                    op0=ALU.mult, op1=ALU.mult),
                      reads=[skey, "gc", ("rst", p)], writes=[dkey])
        units = []
        for I in range((NB + 3) // 4):
            iq0 = 4 * I
            iq1 = min(iq0 + 4, NB)
            for j in range(iq1):
                units.append((I, iq0, iq1, j, j == 0, j == iq1 - 1))

        def stage_a(u, p):
            (I, iq0, iq1, j, first, last) = u
            i0 = max(iq0, j)
            ncols = (iq1 - i0) * 128
            t0 = i0 * 128
            kj = kn[:, j * 128:(j + 1) * 128]
            qcols = qn[:, t0:t0 + ncols]
            diag = (j >= iq0)
            P.add("pe", lambda e: e.matmul(ps[:, BA[p], 0:ncols], lhsT=kj, rhs=qcols, start=True, stop=True),
                  reads=["kn", "qn"], writes=[("bA", p)])
            P.add("act", lambda e: e.activation(out=ebuf[p][:, 0:ncols], in_=ps[:, BA[p], 0:ncols], func=AF.Exp),
                  reads=[("bA", p)], writes=[("e", p)])
            P.add("act", lambda e: e.activation(out=spb[p][:, 0:ncols], in_=ebuf[p][:, 0:ncols], func=AF.Ln, scale=1.0, bias=1.0),
                  reads=[("e", p)], writes=[("sp", p)])
            if diag:
                P.add("pool", lambda e: e.tensor_tensor(out=spb[p][:, 0:128], in0=spb[p][:, 0:128], in1=mdiag[:, :], op=ALU.mult),
                      reads=[("sp", p), "mdiag"], writes=[("sp", p)])
            if j == 0:
                P.add("pool", lambda e: e.tensor_scalar(out=spb[p][:, 0:ncols], in0=spb[p][:, 0:ncols], scalar1=kval[:, 0:1], scalar2=1.0,
                                                        op0=ALU.mult, op1=ALU.mult),
                      reads=[("sp", p), "kval"], writes=[("sp", p)])

        def stage_b(u, p):
            (I, iq0, iq1, j, first, last) = u
            i0 = max(iq0, j)
            ncols = (iq1 - i0) * 128
            t0 = i0 * 128
            nqb = ncols // 128
            kj = kn[:, j * 128:(j + 1) * 128]
            qcols = qn[:, t0:t0 + ncols]
            diag = (j >= iq0)

            def mmB(e):
                e.matmul(ps[:, BB[p], 0:ncols], lhsT=kj, rhs=qcols, start=True, stop=False)
                e.matmul(ps[:, BB[p], 0:ncols], lhsT=ntri[:, :], rhs=spb[p][:, 0:ncols], start=False, stop=True)
                ins = None
                for il in range(nqb):
                    ins = e.matmul(ps[:, 6 + p, il:il + 1], lhsT=spb[p][:, il * 128:(il + 1) * 128], rhs=onec[:, :],
                                   start=True, stop=True)
                return ins
            ckey = ("bC", 0) if p == 0 else "pst"
            P.add("pe", mmB, reads=["kn", "qn", ("sp", p), "ntri", "onec"], writes=[("bB", p), ckey])
            P.add("act", lambda e: e.activation(out=att[p][:, 0:ncols], in_=ps[:, BB[p], 0:ncols], func=AF.Exp),
                  reads=[("bB", p)], writes=[("att", p)])
            P.add("act", lambda e: e.activation(out=dcy[p][:, 0:nqb], in_=ps[:, 6 + p, 0:nqb], func=AF.Exp, scale=-1.0),
                  reads=[ckey], writes=[("dcy", p)])
            if diag:
                P.add("pool", lambda e: e.tensor_tensor(out=att[p][:, 0:128], in0=att[p][:, 0:128], in1=mdiag[:, :], op=ALU.mult),
                      reads=[("att", p), "mdiag"], writes=[("att", p)])
            if j == 0:
                P.add("pool", lambda e: e.tensor_scalar(out=att[p][:, 0:ncols], in0=att[p][:, 0:ncols], scalar1=kval[:, 0:1], scalar2=1.0,
                                                        op0=ALU.mult, op1=ALU.mult),
                      reads=[("att", p), "kval"], writes=[("att", p)])

        def stage_o(u, p, h=h):
            (I, iq0, iq1, j, first, last) = u
            i0 = max(iq0, j)
            nqb = iq1 - i0
            nq = iq1 - iq0
            if first:
                P.add("pool", lambda e: e.memset(acc[:, 0:nq, :], 0.0), writes=["acc"])

            def mmO(e):
                ins = None
                for il in range(nqb):
                    ins = e.matmul(ps[:, BO[p], il * 128:(il + 1) * 128], lhsT=att[p][:, il * 128:(il + 1) * 128],
                                   rhs=v_all[:, j, h * HD:(h + 1) * HD], start=True, stop=True)
                return ins
            P.add("pe", mmO, reads=[("att", p), "v_all"], writes=[("bO", p)])
            for il in range(nqb):
                ia = (i0 - iq0) + il
                P.add("dve", lambda e, il=il, ia=ia: e.scalar_tensor_tensor(
                    out=acc[:, ia, :], in0=acc[:, ia, :], scalar=dcy[p][:, il:il + 1], in1=ps[:, BO[p], il * 128:(il + 1) * 128],
                    op0=ALU.mult, op1=ALU.add),
                      reads=["acc", ("dcy", p), ("bO", p)], writes=["acc"])
            if last:
                P.add("act", lambda e: e.activation(out=o_h[:, iq0:iq1, :], in_=acc[:, 0:nq, :], func=AF.Copy),
                      reads=["acc"], writes=["o_h"])

        nu = len(units)
        base = par[0]
        for step in range(nu + 2):
            if step < nu:
                stage_a(units[step], (base + step) % 2)
            if 1 <= step <= nu:
                stage_b(units[step - 1], (base + step - 1) % 2)
            if step >= 2:
                stage_o(units[step - 2], (base + step - 2) % 2)
        par[0] = (base + nu) % 2
        for b0 in range(0, NB, 8):
            n = min(8, NB - b0)

            def tro(e, b0=b0, n=n):
                ins = None
                for jj in range(n):
                    ins = e.transpose(out=pst[:, jj * 128:(jj + 1) * 128], in_=o_h[:, b0 + jj, :], identity=ident[:, :])
                return ins
            P.add("pe", tro, reads=["o_h", "ident"], writes=["pst"])
            P.add("dve", lambda e, b0=b0, n=n, h=h: e.tensor_copy(out=oT_sb[:, h, b0 * 128:(b0 + n) * 128], in_=pst[:, 0:n * 128]),
                  reads=["pst"], writes=["oT_sb"])
    fin = []
    o_m, o_t = oT_d
    for s in range(4):
        P.dma("sp", "oout", out=o_m[s * 512:(s + 1) * 512, :].rearrange("(h p) t -> p h t", p=128),
              in_=oT_sb[:, :, s * 1056:s * 1056 + 1024], reads=["oT_sb"], writes=[("oout", s, "m")])
        P.dma("sp", "oout", out=o_t[s * 512:(s + 1) * 512, :].rearrange("(h p) t -> p h t", p=128),
              in_=oT_sb[:, :, s * 1056 + 1024:(s + 1) * 1056], reads=["oT_sb"], writes=[("oout", s, "t")])
        fin += [("oout", s, "m"), ("oout", s, "t")]
    return fin


T = 1056
TG = 352
NTG = 3
D = 2048
KT_D = 16
DFF = 5632
SCS = [12, 12, 12, 8]
TTILES = [(i * 128, 128) for i in range(8)] + [(1024, 32)]


class DenseCtx:
    def __init__(self, nc, P, A, ps, wb=256, n_wslots=4):
        self.nc = nc
        self.P = P
        self.psum = ps
        self.wb = wb
        self.wslots = [A.alloc([128, KT_D, wb], BF16) for i in range(n_wslots)]
        self.wi = 0
        self.ones = A.alloc([128, 128], BF16)
        self.sq = [A.alloc([128, T], BF16) for i in range(2)]
        self.sqi = 0
        self.rstd = A.alloc([128, T], F32)
        self.tm = 0
        P.add("pool", lambda e: e.memset(self.ones[:, :], 1.0), writes=[("ones",)])

    def tri_view(self, t):
        return self.psum[:, 3 * t:3 * t + 3, 0:TG]

    def tri_key(self, t):
        return ("ps", t)

    def next_wslot(self):
        i = self.wi
        self.wi = (self.wi + 1) % len(self.wslots)
        return i

    def next_tm_bank(self):
        b = 6 + self.tm
        self.tm ^= 1
        return b


def v3(ap):
    return ap.rearrange("p (g t) -> p g t", g=NTG)


def rms_stats(C, hT, hkey):
    P = C.P
    t = P.next_tri()
    for kt in range(KT_D):
        si = C.sqi
        C.sqi ^= 1
        sq = C.sq[si]
        P.add("act", lambda e, sq=sq, kt=kt: e.activation(out=sq[:, :], in_=hT[:, kt, :], func=AF.Square),
              reads=[hkey(kt)], writes=[("sq", si)])

        def mm(e, sq=sq, kt=kt, t=t):
            ins = None
            for g in range(NTG):
                ins = e.matmul(C.psum[:, 3 * t + g, 0:TG], lhsT=C.ones[:, :], rhs=sq[:, g * TG:(g + 1) * TG],
                               start=(kt == 0), stop=(kt == KT_D - 1))
            return ins
        P.add("pe", mm, reads=[("sq", si), ("ones",)], writes=[C.tri_key(t)])
    P.add("act", lambda e: e.activation(out=v3(C.rstd[:, :]), in_=C.tri_view(t), func=AF.Sqrt, scale=1.0 / D, bias=1e-6),
          reads=[C.tri_key(t)], writes=[("rstd",)])
    P.add("dve", lambda e: e.reciprocal(out=C.rstd[:, :], in_=C.rstd[:, :]), reads=[("rstd",)], writes=[("rstd",)])


def apply_norm(C, hT, hkey, gcol, gkey, yT, ykey):
    P = C.P
    for kt in range(KT_D):
        sc_ = 1.0 if gcol is None else gcol[:, kt:kt + 1]
        rd = [hkey(kt), ("rstd",)] + ([] if gcol is None else [gkey])
        P.add("dve", lambda e, kt=kt, sc_=sc_: e.scalar_tensor_tensor(out=yT[:, kt, :], in0=hT[:, kt, :], scalar=sc_, in1=C.rstd[:, :],
                                                                      op0=ALU.mult, op1=ALU.mult),
              reads=rd, writes=[ykey(kt)])


def load_wblock(C, w_dram, row0, KT, c0, cw):
    ws = C.next_wslot()
    wt = C.wslots[ws]
    src = w_dram[row0:row0 + KT * 128, c0:c0 + cw].rearrange("(kt p) n -> p kt n", p=128)
    C.P.dma("pool", f"w{ws}", out=wt[:, 0:KT, 0:cw], in_=src, writes=[("w", ws)])
    return ws


def mm_group(C, wt, wkey, j, rows, xT, xkeys, KT):
    P = C.P
    t = P.next_tri()

    def mm(e):
        ins = None
        for kt in range(KT):
            for g in range(NTG):
                ins = e.matmul(C.psum[0:rows, 3 * t + g, 0:TG], lhsT=wt[:, kt, j * 128:j * 128 + rows],
                               rhs=xT(kt)[:, g * TG:(g + 1) * TG], start=(kt == 0), stop=(kt == KT - 1))
        return ins
    P.add("pe", mm, reads=[wkey] + xkeys, writes=[C.tri_key(t)])
    return t


def dense(C, xT, xkey, KT, w_dram, row0, col0, ncols, epilogue):
    wb = C.wb
    nt = 0
    xkeys = [xkey(kt) for kt in range(KT)]
    for b in range((ncols + wb - 1) // wb):
        c0 = col0 + b * wb
        cw = min(wb, col0 + ncols - c0)
        ws = load_wblock(C, w_dram, row0, KT, c0, cw)
        for j in range((cw + 127) // 128):
            rows = min(128, cw - j * 128)
            t = mm_group(C, C.wslots[ws], ("w", ws), j, rows, xT, xkeys, KT)
            epilogue(nt, rows, t)
            nt += 1


def dense_tm(C, yT, ykey, wts, wkeys, epilogue):
    P = C.P
    ykeys = [ykey(kt) for kt in range(KT_D)]
    for tt, (t0, rows) in enumerate(TTILES):
        bank = C.next_tm_bank()

        def mm(e, t0=t0, rows=rows, bank=bank):
            ins = None
            for bi, wt in enumerate(wts):
                for kt in range(KT_D):
                    ins = e.matmul(C.psum[0:rows, bank, bi * 256:(bi + 1) * 256], lhsT=yT[:, kt, t0:t0 + rows], rhs=wt[:, kt, 0:256],
                                   start=(kt == 0), stop=(kt == KT_D - 1))
            return ins
        P.add("pe", mm, reads=list(wkeys) + ykeys, writes=[("tmb", bank)])
        epilogue(tt, t0, rows, bank)


def load_tokens(P, slot, dst, src_ap, keyf):
    for q in range(4):
        P.dma("sp", slot, out=dst[:, 4 * q:4 * q + 4, :], in_=src_ap(512 * q, 512 * (q + 1)).rearrange("(kt p) t -> p kt t", p=128),
              writes=[keyf(kt) for kt in range(4 * q, 4 * q + 4)])


hkey = lambda kt: ("h", kt)
ykey = lambda kt: ("y", kt)
ukey = lambda kt: ("u", kt)


def inproj_phase(nc, P, A, ps, xbT_d, w_d, g_d, qT_s, kT_s, gT_s, v_s, aT_s):
    C = DenseCtx(nc, P, A, ps)
    hT = A.alloc([128, KT_D, T], F32)
    yT = A.alloc([128, KT_D, T], BF16)
    gcol = A.alloc([128, 16], F32)
    outs = [A.alloc([128, T], BF16) for i in range(4)]
    vst = [A.alloc([128, 512], BF16) for i in range(2)]
    a_sb = A.alloc([16, T], F32)
    P.dma("sp", "g", out=gcol[:, :], in_=g_d[:, :], writes=[("g",)])
    st = {"i": 0, "v": 0}
    fin = []
    load_tokens(P, "hin", hT, lambda r0, r1: xbT_d[r0:r1, 0:T], hkey)
    for tg in range(4):
        c0 = tg * T
        rms_stats(C, hT, hkey)
        apply_norm(C, hT, hkey, gcol, ("g",), yT, ykey)
        if tg < 3:
            load_tokens(P, "hin", hT, lambda r0, r1, c1=c0 + T: xbT_d[r0:r1, c1:c1 + T], hkey)

        def epi_fm(dst, row0, tag):
            def epi(nt, rows, t):
                i = st["i"] % 4
                st["i"] += 1
                o = outs[i]
                if st["i"] % 2 == 0:
                    P.add("act", lambda e, o=o, t=t: e.activation(out=v3(o[:, :]), in_=C.tri_view(t), func=AF.Copy),
                          reads=[C.tri_key(t)], writes=[("o", i)])
                else:
                    P.add("dve", lambda e, o=o, t=t: e.tensor_copy(out=v3(o[:, :]), in_=C.tri_view(t)),
                          reads=[C.tri_key(t)], writes=[("o", i)])
                P.dma("sp", f"o{i}", out=dst[row0 + nt * 128:row0 + (nt + 1) * 128, c0:c0 + T], in_=o[:, :], reads=[("o", i)],
                      writes=[(tag, tg, nt)])
                fin.append((tag, tg, nt))
            return epi
        dense(C, lambda kt: yT[:, kt, :], ykey, KT_D, w_d, 0, 0, 256, epi_fm(qT_s, 0, "qs"))
        dense(C, lambda kt: yT[:, kt, :], ykey, KT_D, w_d, 0, 256, 256, epi_fm(kT_s, 0, "ks"))
        dense(C, lambda kt: yT[:, kt, :], ykey, KT_D, w_d, 0, 1024, 512, epi_fm(gT_s, 0, "gs"))

        def epi_a(nt, rows, t):
            P.add("dve", lambda e, t=t: e.tensor_copy(out=v3(a_sb[:, :]), in_=C.psum[0:16, 3 * t:3 * t + 3, 0:TG]),
                  reads=[C.tri_key(t)], writes=[("a",)])
            P.dma("sp", "aout", out=aT_s[:, c0:c0 + T], in_=a_sb[:, :], reads=[("a",)], writes=[("as", tg)])
            fin.append(("as", tg))
        dense(C, lambda kt: yT[:, kt, :], ykey, KT_D, w_d, 0, 1536, 16, epi_a)
        ws0 = load_wblock(C, w_d, 0, KT_D, 512, 256)
        ws1 = load_wblock(C, w_d, 0, KT_D, 768, 256)

        def epi_v(tt, t0, rows, bank):
            i = st["v"]
            st["v"] ^= 1
            P.add("act", lambda e, i=i, rows=rows, bank=bank: e.activation(out=vst[i][0:rows, :], in_=C.psum[0:rows, bank, :], func=AF.Copy),
                  reads=[("tmb", bank)], writes=[("vst", i)])
            P.dma("sp", f"vs{i}", out=v_s[c0 + t0:c0 + t0 + rows, :], in_=vst[i][0:rows, :], reads=[("vst", i)], writes=[("vs", tg, tt)])
            fin.append(("vs", tg, tt))
        dense_tm(C, yT, ykey, [C.wslots[ws0], C.wslots[ws1]], [("w", ws0), ("w", ws1)], epi_v)
    return fin


def ffn_phase(nc, P, A, ps, kind, h_src, o_all_d, wmix_d, wg_d, wu_d, wd_d, g_d, hout_d, xh_loc=None):
    C = DenseCtx(nc, P, A, ps)
    hT = A.alloc([128, KT_D, T], F32)
    yT = A.alloc([128, KT_D, T], BF16)
    uT = A.alloc([128, 12, T], BF16)
    gcols = A.alloc([128, 16], F32)
    sg = [A.alloc([128, T], F32) for i in range(2)]
    P.dma("sp", "g", out=gcols[:, :], in_=g_d[:, :], writes=[("g",)])
    oa_m, oa_t = o_all_d
    for q in range(4):
        def ldm(e, q=q):
            if "rowoff_m" not in P.cache:
                P.cache["rowoff_m"] = e.snap((e.partition_id() % 4) * 2048)
            return e.dma_start(out=yT[:, 4 * q:4 * q + 4, 0:1024],
                               in_=oa_m[bass.ds(P.cache["rowoff_m"] + q * 512, 512), :].rearrange("(kt p) t -> p kt t", p=128))
        P.add("sp", ldm, reads=[(("o_all",), ci) for ci in range(5)], writes=[ykey(kt) for kt in range(4 * q, 4 * q + 4)], dma="oin")

        def ldt(e, q=q):
            if "rowoff_t" not in P.cache:
                P.cache["rowoff_t"] = e.snap((e.partition_id() % 4) * 512)
            return e.dma_start(out=yT[:, 4 * q:4 * q + 4, 1024:1056],
                               in_=oa_t[bass.ds(P.cache["rowoff_t"] + q * 2048, 512), :].rearrange("(kt p) t -> p kt t", p=128))
        P.add("sp", ldt, reads=[(("o_all",), ci) for ci in range(5)], writes=[ykey(kt) for kt in range(4 * q, 4 * q + 4)], dma="oin")
    load_tokens(P, "hin", hT, lambda r0, r1: h_src[r0:r1, :], hkey)

    def epi_resid(nt, rows, t):
        P.add("dve", lambda e, nt=nt, t=t: e.tensor_tensor(out=v3(hT[:, nt, :]), in0=v3(hT[:, nt, :]), in1=C.tri_view(t), op=ALU.add),
              reads=[C.tri_key(t), hkey(nt)], writes=[hkey(nt)])
    dense(C, lambda kt: yT[:, kt, :], ykey, KT_D, wmix_d, 0, 0, D, epi_resid)
    rms_stats(C, hT, hkey)
    apply_norm(C, hT, hkey, gcols, ("g",), yT, ykey)
    sgi = [0]
    f_nt = 0
    xkeys = [ykey(kt) for kt in range(KT_D)]
    for sc_n in SCS:
        for blk in range(sc_n // 2):
            c0 = (f_nt + 2 * blk) * 128
            wsg = load_wblock(C, wg_d, 0, KT_D, c0, 256)
            wsu = load_wblock(C, wu_d, 0, KT_D, c0, 256)
            for j in range(2):
                jj = 2 * blk + j
                tg_ = mm_group(C, C.wslots[wsg], ("w", wsg), j, 128, lambda kt: yT[:, kt, :], xkeys, KT_D)
                si = sgi[0]
                sgi[0] ^= 1
                P.add("act", lambda e, si=si, t=tg_: e.activation(out=v3(sg[si][:, :]), in_=C.tri_view(t), func=AF.Silu),
                      reads=[C.tri_key(tg_)], writes=[("sg", si)])
                tu_ = mm_group(C, C.wslots[wsu], ("w", wsu), j, 128, lambda kt: yT[:, kt, :], xkeys, KT_D)
                P.add("dve", lambda e, si=si, t=tu_, jj=jj: e.tensor_tensor(out=v3(uT[:, jj, :]), in0=v3(sg[si][:, :]), in1=C.tri_view(t),
                                                                            op=ALU.mult),
                      reads=[C.tri_key(tu_), ("sg", si)], writes=[ukey(jj)])
        dense(C, lambda kt: uT[:, kt, :], ukey, sc_n, wd_d, f_nt * 128, 0, D, epi_resid)
        f_nt += sc_n
    fin = []
    for q in range(4):
        P.dma("sp", "hout", out=hout_d[512 * q:512 * (q + 1), :].rearrange("(kt p) t -> p kt t", p=128), in_=hT[:, 4 * q:4 * q + 4, :],
              reads=[hkey(kt) for kt in range(4 * q, 4 * q + 4)], writes=[("hout", q)])
        fin.append(("hout", q))
    if kind == "A":
        rms_stats(C, hT, hkey)
        apply_norm(C, hT, hkey, None, None, yT, ykey)
        xh_m, xh_t = xh_loc
        for q in range(4):
            P.dma("sp", "xh", out=xh_m[512 * q:512 * (q + 1), :].rearrange("(kt p) t -> p kt t", p=128), in_=yT[:, 4 * q:4 * q + 4, 0:1024],
                  reads=[ykey(kt) for kt in range(4 * q, 4 * q + 4)], writes=[("xh", q, "m")])
            P.dma("sp", "xh", out=xh_t[512 * q:512 * (q + 1), :].rearrange("(kt p) t -> p kt t", p=128), in_=yT[:, 4 * q:4 * q + 4, 1024:1056],
                  reads=[ykey(kt) for kt in range(4 * q, 4 * q + 4)], writes=[("xh", q, "t")])
            fin += [("xh", q, "m"), ("xh", q, "t")]
    return fin


def qkv_phase(nc, P, A, ps, xh_all_d, w_d, g_d, q4T_s, k4T_s, v4_s):
    C = DenseCtx(nc, P, A, ps, n_wslots=6)
    yT = A.alloc([128, KT_D, T], BF16)
    gcols = A.alloc([128, 32], F32)
    outs = [A.alloc([128, T], BF16) for i in range(4)]
    vst = [A.alloc([128, 512], BF16) for i in range(2)]
    P.dma("sp", "g", out=gcols[:, :], in_=g_d[:, :], writes=[("g",)])
    for b in range(6):
        ws = load_wblock(C, w_d, 0, KT_D, b * 256, 256)
        assert ws == b
        goff = 16 if b < 2 else 0
        P.add("pool", lambda e, b=b, goff=goff: e.tensor_tensor(out=C.wslots[b][:, :, :], in0=C.wslots[b][:, :, :],
                                                                in1=gcols[:, goff:goff + 16].unsqueeze(2).to_broadcast([128, 16, 256]),
                                                                op=ALU.mult),
              reads=[("w", b), ("g",)], writes=[("w", b)])
    st = {"i": 0, "v": 0}
    fin = []
    xkeys = [ykey(kt) for kt in range(KT_D)]
    for tg in range(4):
        c0 = tg * T
        xa_m, xa_t = xh_all_d
        for q in range(4):
            P.dma("sp", "xin", out=yT[:, 4 * q:4 * q + 4, 0:1024],
                  in_=xa_m[q * 2048 + tg * 512:q * 2048 + (tg + 1) * 512, :].rearrange("(kt p) t -> p kt t", p=128),
                  reads=[(("xh_all",), c5) for c5 in range(5)], writes=[ykey(kt) for kt in range(4 * q, 4 * q + 4)])
            P.dma("sp", "xin", out=yT[:, 4 * q:4 * q + 4, 1024:1056],
                  in_=xa_t[tg * 2048 + q * 512:tg * 2048 + (q + 1) * 512, :].rearrange("(kt p) t -> p kt t", p=128),
                  reads=[(("xh_all",), c5) for c5 in range(5)], writes=[ykey(kt) for kt in range(4 * q, 4 * q + 4)])
        for (dst, tag, blocks) in ((q4T_s, "q4", (0, 1)), (k4T_s, "k4", (2, 3))):
            for bi, b in enumerate(blocks):
                for j in range(2):
                    nt = bi * 2 + j
                    t = mm_group(C, C.wslots[b], ("w", b), j, 128, lambda kt: yT[:, kt, :], xkeys, KT_D)
                    i = st["i"] % 4
                    st["i"] += 1
                    o = outs[i]
                    if st["i"] % 2 == 0:
                        P.add("act", lambda e, o=o, t=t: e.activation(out=v3(o[:, :]), in_=C.tri_view(t), func=AF.Copy),
                              reads=[C.tri_key(t)], writes=[("o", i)])
                    else:
                        P.add("dve", lambda e, o=o, t=t: e.tensor_copy(out=v3(o[:, :]), in_=C.tri_view(t)),
                              reads=[C.tri_key(t)], writes=[("o", i)])
                    P.dma("sp", f"o{i}", out=dst[nt * 128:(nt + 1) * 128, c0:c0 + T], in_=o[:, :], reads=[("o", i)], writes=[(tag, tg, nt)])
                    fin.append((tag, tg, nt))

        def epi_v(tt, t0, rows, bank):
            i = st["v"]
            st["v"] ^= 1
            P.add("act", lambda e, i=i, rows=rows, bank=bank: e.activation(out=vst[i][0:rows, :], in_=C.psum[0:rows, bank, :], func=AF.Copy),
                  reads=[("tmb", bank)], writes=[("vst", i)])
            P.dma("sp", f"vs{i}", out=v4_s[c0 + t0:c0 + t0 + rows, :], in_=vst[i][0:rows, :], reads=[("vst", i)], writes=[("v4", tg, tt)])
            fin.append(("v4", tg, tt))
        dense_tm(C, yT, ykey, [C.wslots[4], C.wslots[5]], [("w", 4), ("w", 5)], epi_v)
    return fin


def build_fused(stop=99):
    nc = bass.Bass("TRN2", target_bir_lowering=False)
    dt_in = lambda name, shape, dt=F32: nc.dram_tensor(name, shape, dt, kind="ExternalInput")
    scr = lambda name, shape, dt: nc.dram_tensor(name, shape, dt)
    RG = [[0, 1, 2, 3], [4, 5, 6, 7]]
    P = Prog(nc)
    A = Arena(nc, 204 * 1024)
    ps = nc.alloc_psum_tensor("ps", [128, 8, 512], F32)

    def gather5(slot, src, dst, rkeys, wkey):
        (src_m, src_t), (dst_m, dst_t) = src, dst
        for ci in range(4):
            P.coll(slot, lambda e, ci=ci: e.collective_compute("AllGather", ALU.bypass, replica_groups=RG,
                                                               ins=[src_m[ci * 512:(ci + 1) * 512, :]],
                                                               outs=[dst_m[ci * 2048:(ci + 1) * 2048, :]]),
                   reads=rkeys, writes=[(wkey, ci)])
        P.coll(slot, lambda e: e.collective_compute("AllGather", ALU.bypass, replica_groups=RG, ins=[src_t.ap()], outs=[dst_t.ap()]),
               reads=rkeys, writes=[(wkey, 4)])

    def finish(keys, tap=None):
        if tap is not None:
            src, shape, dt = tap
            dbg = nc.dram_tensor("dbg", shape, dt, kind="ExternalOutput")
            P.dma("sp", "dbg", out=dbg.ap(), in_=(src if hasattr(src, "tensor") else src.ap()), reads=keys, writes=[("dbg",)])
            keys = [("dbg",)]
        for en in ("sp", "pool", "act", "dve", "pe"):
            P.add(en, None, reads=keys)
        P.build()
        return nc, P

    xbT_d = dt_in("xbT", [D, L])
    w_in_d = dt_in("w_in_h", [D, 1552])
    gA_d = dt_in("gcolA", [128, 16])
    wgu_d = dt_in("wgu", [16, DK])
    colsA_d = dt_in("colsA", [128, 8])
    qT_s = scr("qT_s", [DK, L], BF16); kT_s = scr("kT_s", [DK, L], BF16); gT_s = scr("gT_s", [DV, L], BF16)
    v_s = scr("v_s", [L, DV], BF16); aT_s = scr("aT_s", [16, L], F32)
    og_loc = (scr("og_loc_m", [4 * DV, 1024], BF16), scr("og_loc_t", [4 * DV, 32], BF16))
    og_all = (scr("og_all_m", [16 * DV, 1024], BF16), scr("og_all_t", [16 * DV, 32], BF16))
    fin = inproj_phase(nc, P, A, ps, xbT_d, w_in_d, gA_d, qT_s, kT_s, gT_s, v_s, aT_s)
    if stop == 0:
        return finish(fin, (gT_s, [DV, L], BF16))
    P.barrier(); A.reset(0)
    og_state = {"next": 0, "keys": []}

    def on_store(g, keys):
        if stop == 1:
            return
        og_state["keys"] += list(keys)
        upto = sum(1 for s in range(4) if (s * 1056 + 1023) < (g + 1) * GT)
        for ci in range(og_state["next"], upto):
            if g == NG - 1:
                og_state.setdefault("deferred", []).append(ci)
                continue
            P.coll("cc1", lambda e, ci=ci: e.collective_compute("AllGather", ALU.bypass, replica_groups=RG,
                                                                ins=[og_loc[0][ci * 512:(ci + 1) * 512, :]],
                                                                outs=[og_all[0][ci * 2048:(ci + 1) * 2048, :]]),
                   reads=list(og_state["keys"]), writes=[(("o_all",), ci)])
        og_state["next"] = upto
    fin = gla_phase(nc, P, A, ps, qT_s, kT_s, v_s, gT_s, aT_s, wgu_d, colsA_d, og_loc, on_store)
    if stop == 1:
        return finish(fin, (og_loc[0], [4 * DV, 1024], BF16))
    assert og_state["next"] == 4
    P.barrier(); A.reset(0)
    for ci in og_state.get("deferred", []):
        P.coll("cc1", lambda e, ci=ci: e.collective_compute("AllGather", ALU.bypass, replica_groups=RG,
                                                            ins=[og_loc[0][ci * 512:(ci + 1) * 512, :]],
                                                            outs=[og_all[0][ci * 2048:(ci + 1) * 2048, :]]),
               writes=[(("o_all",), ci)])
    P.coll("cc1", lambda e: e.collective_compute("AllGather", ALU.bypass, replica_groups=RG, ins=[og_loc[1].ap()], outs=[og_all[1].ap()]),
           writes=[(("o_all",), 4)])
    hT_d = dt_in("hT", [D, T])
    w_out_d = dt_in("w_out", [D, D])
    wg0_d = dt_in("w_gate0", [D, DFF]); wu0_d = dt_in("w_up0", [D, DFF]); wd0_d = dt_in("w_down0", [DFF, D])
    gF0_d = dt_in("gF0", [128, 16])
    h2_s = scr("h2_s", [D, T], F32)
    xh_loc = (scr("xh_loc_m", [D, 1024], BF16), scr("xh_loc_t", [D, 32], BF16))
    xh_all = (scr("xh_all_m", [4 * D, 1024], BF16), scr("xh_all_t", [4 * D, 32], BF16))
    fin = ffn_phase(nc, P, A, ps, "A", hT_d, og_all, w_out_d, wg0_d, wu0_d, wd0_d, gF0_d, h2_s, xh_loc)
    if stop == 3:
        return finish(fin, (h2_s, [D, T], F32))
    P.barrier(); A.reset(0)
    gather5("cc2", xh_loc, xh_all, [], ("xh_all",))
    w_qkv_d = dt_in("w_qkv_h", [D, 1536])
    gQKV_d = dt_in("gQKV", [128, 32])
    gqk_d = dt_in("gqk", [128, 2])
    q4T_s = scr("q4T_s", [512, L], BF16); k4T_s = scr("k4T_s", [512, L], BF16); v4_s = scr("v4_s", [L, 512], BF16)
    o_loc = (scr("o_loc_m", [4 * 512, 1024], BF16), scr("o_loc_t", [4 * 512, 32], BF16))
    o_all = (scr("o_all_m", [16 * 512, 1024], BF16), scr("o_all_t", [16 * 512, 32], BF16))
    fin = qkv_phase(nc, P, A, ps, xh_all, w_qkv_d, gQKV_d, q4T_s, k4T_s, v4_s)
    if stop == 4:
        return finish(fin, (k4T_s, [512, L], BF16))
    P.barrier(); A.reset(0)
    fin = sb_phase(nc, P, A, ps, q4T_s, k4T_s, v4_s, gqk_d, o_loc)
    if stop == 5:
        return finish(fin, (o_loc[0], [4 * 512, 1024], BF16))
    P.barrier(); A.reset(0)
    gather5("cc3", o_loc, o_all, [], ("o_all",))
    w_o_d = dt_in("w_o", [D, D])
    wg1_d = dt_in("w_gate1", [D, DFF]); wu1_d = dt_in("w_up1", [D, DFF]); wd1_d = dt_in("w_down1", [DFF, D])
    gF1_d = dt_in("gF1", [128, 16])
    out_d = nc.dram_tensor("hout", [D, T], F32, kind="ExternalOutput")
    fin = ffn_phase(nc, P, A, ps, "B", h2_s, o_all, w_o_d, wg1_d, wu1_d, wd1_d, gF1_d, out_d)
    return finish(fin)


_PROG = []
_STOP = [99]


def _cols(vec):
    v = np.asarray(vec, dtype=np.float32)
    return np.ascontiguousarray(v.reshape(-1, 128).T)


def kernel(x, meta_tokens, g_norm_a, w_in_a, w_gate_up_a, b_gate_a, g_onorm_a, w_out_a,
           g_kv_norm, w_kv, g_k, g_norm_b, w_q_b, g_q_b, w_o_b,
           g_ffn_norm, w_ffn_gate, w_ffn_up, w_ffn_down):
    f32 = np.float32
    C = np.ascontiguousarray
    x = np.asarray(x, f32)
    B = x.shape[0]
    meta = np.asarray(meta_tokens, f32)
    h0 = np.concatenate([np.zeros((B, 112, D), f32), np.broadcast_to(meta[None], (B, 16, D)), x], axis=1)
    xbT = [C(h0[b].T) for b in range(B)]
    w_in = np.asarray(w_in_a, f32)[0]
    wgu_full = np.asarray(w_gate_up_a, f32)[0]
    bg = np.asarray(b_gate_a, f32)[0]
    gon = np.asarray(g_onorm_a, f32)[0]
    wkv = np.asarray(w_kv, f32)
    wq = np.asarray(w_q_b, f32)[0]
    gffn = np.asarray(g_ffn_norm, f32)
    shared = {
        "gcolA": _cols(np.asarray(g_norm_a)[0]),
        "w_out": C(np.asarray(w_out_a, f32)[0]),
        "w_gate0": C(np.asarray(w_ffn_gate, f32)[0]), "w_up0": C(np.asarray(w_ffn_up, f32)[0]), "w_down0": C(np.asarray(w_ffn_down, f32)[0]),
        "gF0": _cols(gffn[0]),
        "gQKV": C(np.concatenate([_cols(np.asarray(g_kv_norm)), _cols(np.asarray(g_norm_b)[0])], axis=1)),
        "gqk": C(np.stack([np.asarray(g_q_b, f32)[0], np.asarray(g_k, f32)], axis=1)),
        "w_o": C(np.asarray(w_o_b, f32)[0]),
        "w_gate1": C(np.asarray(w_ffn_gate, f32)[1]), "w_up1": C(np.asarray(w_ffn_up, f32)[1]), "w_down1": C(np.asarray(w_ffn_down, f32)[1]),
        "gF1": _cols(gffn[1]),
    }
    in_maps = []
    for c in range(8):
        b, i = divmod(c, 4)
        m = dict(shared)
        m["xbT"] = xbT[b]
        m["hT"] = C(xbT[b][:, i * T:(i + 1) * T])
        m["w_in_h"] = C(np.concatenate([w_in[:, i * DK:(i + 1) * DK], w_in[:, 1024 + i * DK:1024 + (i + 1) * DK],
                                        w_in[:, 2048 + i * DV:2048 + (i + 1) * DV], w_in[:, 4096 + i * DV:4096 + (i + 1) * DV],
                                        w_in[:, 6144:6160]], axis=1))
        m["wgu"] = C(wgu_full[:, i * DK:(i + 1) * DK])
        cols = np.zeros((128, 8), f32)
        cols[:, 0:2] = bg[i * DK:(i + 1) * DK].reshape(2, 128).T
        cols[:, 2:6] = gon.reshape(4, 128).T
        m["colsA"] = cols
        m["w_qkv_h"] = C(np.concatenate([wq[:, i * 512:(i + 1) * 512], wkv[:, i * 512:(i + 1) * 512],
                                         wkv[:, 2048 + i * 512:2048 + (i + 1) * 512]], axis=1))
        in_maps.append(m)
    if _STOP[0] != 99:
        return in_maps
    if not _PROG:
        _PROG.append(build_fused()[0])
    res = run_bass_kernel_spmd(_PROG[0], in_maps, core_ids=list(range(8)))
    r = res.results
    hfin = np.concatenate([np.asarray(r[c]["hout"]).T for c in range(8)], axis=0).reshape(B, L, D)
    return np.ascontiguousarray(hfin[:, 128:, :].astype(np.float32))
```

```python
import numpy as np
import ml_dtypes
from concourse.bass_utils import run_bass_kernel_spmd
import numpy as np
import concourse.bass as bass
import concourse.mybir as mybir

F32 = mybir.dt.float32
BF16 = mybir.dt.bfloat16
AF = mybir.ActivationFunctionType
ALU = mybir.AluOpType
AX = mybir.AxisListType

ENGS = ("pe", "act", "dve", "pool", "sp")


class Op:
    __slots__ = ("idx", "eng", "fn", "reads", "writes", "dma", "deps", "signal", "count",
                 "pre_wait", "group", "waits", "inc", "barrier")

    def __init__(self, idx, eng, fn, reads, writes, dma):
        self.idx = idx
        self.eng = eng
        self.fn = fn
        self.reads = tuple(reads)
        self.writes = tuple(writes)
        self.dma = dma
        self.deps = []
        self.signal = False
        self.count = None
        self.pre_wait = None
        self.group = None
        self.waits = []
        self.inc = 16
        self.barrier = False


class Prog:
    def __init__(self, nc):
        self.nc = nc
        self.ops = []
        self.tri = 0
        self.cache = {}

    def add(self, eng, fn, reads=(), writes=(), dma=None):
        op = Op(len(self.ops), eng, fn, reads, writes, dma)
        self.ops.append(op)
        return op

    def dma(self, eng, slot, out, in_, reads=(), writes=()):
        def fn(e, out=out, in_=in_):
            return e.dma_start(out=out, in_=in_)
        return self.add(eng, fn, reads, writes, dma=slot)

    def coll(self, slot, fn, reads=(), writes=()):
        op = self.add("pool", fn, reads, writes, dma=slot)
        op.inc = 1
        return op

    def barrier(self):
        op = self.add("sp", lambda e: e.nop(), (), ())
        op.barrier = True
        return op

    def next_tri(self):
        t = self.tri
        self.tri ^= 1
        return t

    def build(self):
        nc = self.nc
        ops = self.ops
        last_writer = {}
        readers = {}
        dcount = {}
        dgroup = {}
        dclosed = {}
        gend = {}
        cur_barrier = None
        for op in ops:
            deps = set()
            if op.barrier:
                for k, w in last_writer.items():
                    deps.add(w)
                for k, rs in readers.items():
                    deps.update(rs)
                last_writer = {}
                readers = {}
            elif cur_barrier is not None:
                deps.add(cur_barrier)
            for k in op.reads:
                if k in last_writer:
                    deps.add(last_writer[k])
            for k in op.writes:
                if k in last_writer:
                    deps.add(last_writer[k])
                for r in readers.get(k, ()):
                    deps.add(r)
            deps.discard(op.idx)
            op.deps = sorted(deps)
            for di in op.deps:
                p = ops[di]
                if p.dma is not None:
                    s = p.dma
                    if p.group == dgroup[s]:
                        dclosed[s] = True
                        val = dcount[s]
                        gend[(s, p.group)] = val
                    else:
                        val = gend[(s, p.group)]
                    op.waits.append((("dma", s), val))
                else:
                    if p.eng == "pe" and op.eng == "pe" and op.dma is None:
                        continue
                    p.signal = True
                    op.waits.append((("eng", p.eng), di))
            if op.dma is not None:
                s = op.dma
                if s not in dcount:
                    dcount[s] = 0
                    dgroup[s] = 0
                    dclosed[s] = False
                if dclosed[s]:
                    gend[(s, dgroup[s])] = dcount[s]
                    op.pre_wait = (("dma", s), dcount[s])
                    dgroup[s] += 1
                    dclosed[s] = False
                dcount[s] += op.inc
                op.group = dgroup[s]
            if op.barrier:
                cur_barrier = op.idx
            for k in op.reads:
                readers.setdefault(k, []).append(op.idx)
            for k in op.writes:
                last_writer[k] = op.idx
                readers[k] = []
        cnt = {e: 0 for e in ENGS}
        for op in ops:
            if op.dma is None and op.signal:
                cnt[op.eng] += 1
                op.count = cnt[op.eng]
        self.max_counts = dict(cnt)
        sems = {}
        for e in ENGS:
            if cnt[e] > 0:
                sems[("eng", e)] = nc.alloc_semaphore("s_" + e)
        for s in dcount:
            sems[("dma", s)] = nc.alloc_semaphore("d_" + str(s))
        self.sems = sems
        per_eng = {e: [op for op in ops if op.eng == e] for e in ENGS}

        def emit(ename, eng):
            seen = {}
            for op in per_eng[ename]:
                ws = []
                if op.pre_wait is not None:
                    ws.append(op.pre_wait)
                for (sk, v) in op.waits:
                    if sk[0] == "eng":
                        v = ops[v].count
                    ws.append((sk, v))
                for (sk, v) in ws:
                    if v <= seen.get(sk, 0):
                        continue
                    seen[sk] = v
                    eng.wait_ge(sems[sk], v)
                if op.fn is None:
                    continue
                ins = op.fn(eng)
                if op.dma is not None:
                    ins.then_inc(sems[("dma", op.dma)], op.inc)
                elif op.signal:
                    ins.then_inc(sems[("eng", ename)], 1)

        with nc.Block() as block:
            @block.tensor
            def _(e):
                emit("pe", e)

            @block.scalar
            def _(e):
                emit("act", e)

            @block.vector
            def _(e):
                emit("dve", e)

            @block.gpsimd
            def _(e):
                emit("pool", e)

            @block.sync
            def _(e):
                emit("sp", e)


class Arena:
    def __init__(self, nc, nbytes):
        self.n16 = nbytes // 2
        self.t = nc.alloc_sbuf_tensor("arena", [128, self.n16], BF16)
        self.off = 0

    def mark(self):
        return self.off

    def reset(self, off):
        self.off = off

    def alloc(self, shape, dtype):
        n = 1
        for s in shape[1:]:
            n *= s
        e16 = n * (2 if dtype == F32 else 1)
        e16 = (e16 + 15) // 16 * 16
        assert self.off + e16 <= self.n16, ("arena overflow", self.off, e16, self.n16)
        ap = self.t[0:shape[0], self.off:self.off + (n * (2 if dtype == F32 else 1))]
        self.off += e16
        if dtype == F32:
            ap = ap.bitcast(F32)
        if len(shape) == 3:
            ap = ap.rearrange("p (a b) -> p a b", b=shape[2])
        return ap


L = 4224
NB = 33
GB = 11
NG = 3
GT = GB * 128
GC = GB * 2
DK = 256
DV = 512
EPS = 1e-6


def gla_phase(nc, P, A, ps, qT_d, kT_d, v_d, gT_d, aT_d, wgu_d, cols_d, og_d, on_store=None):
    pst = ps[:, 7, :].bitcast(BF16)
    B_Z, B_UP, B_U0, B_U1, B_O0, B_O1, B_ST = 0, 1, 2, 3, 4, 5, 6
    sb = lambda name, shape, dt: A.alloc(shape, dt)
    ones = sb("ones", [128, 128], BF16)
    ident = sb("ident", [128, 128], BF16)
    mlo = sb("mlo", [64, 64], F32)
    mup = sb("mup", [64, 64], F32)
    cols = sb("cols", [128, 8], F32)
    wgu = sb("wgu", [16, DK], F32)
    aT = sb("aT", [16, GT], F32)
    qT = sb("qT", [128, 2, GT], BF16)
    kT = sb("kT", [128, 2, GT], BF16)
    v64 = sb("v64", [64, GC, DV], BF16)
    gs = sb("gs", [128, 4, GT], BF16)
    et = sb("et", [128, GT], F32)
    cc = sb("cc", [128, GT], F32)
    dd = sb("dd", [128, GT], F32)
    ep = sb("ep", [128, GT], F32)
    em = sb("em", [128, GT], F32)
    rmask = sb("rmask", [128, GT], F32)
    al = sb("al", [128, GC], F32)
    be = sb("be", [128, GC], F32)
    dec = sb("dec", [128, 2, GC], F32)
    tsm = sb("tsm", [128, GC], F32)
    qa = sb("qa", [128, 2, GT], BF16)
    qb = sb("qb", [128, 2, GT], BF16)
    ka = sb("ka", [128, 2, GT], BF16)
    kb = sb("kb", [128, 2, GT], BF16)
    qi = sb("qi", [128, 2, GT], BF16)
    ksT = sb("ksT", [128, 2, GT], BF16)
    ks64 = sb("ks64", [64, GC, DK], BF16)
    sc = sb("sc", [64, GC, 64], BF16)
    t1 = sb("t1", [64, 8, 64], F32)
    t2 = sb("t2", [64, 8, 64], F32)
    S = sb("S", [128, 2, DV], F32)
    Sbf = sb("Sbf", [128, 2, DV], BF16)
    sq = sb("sq", [128, 4, 128], BF16)
    rs = sb("rs", [128, 128], F32)
    tmp = sb("tmp", [128, 4, 128], F32)

    P.add("pool", lambda e: e.memset(ones[:, :], 1.0), writes=["ones"])
    P.add("pool", lambda e: e.memset(ident[:, :], 1.0), writes=["ident"])
    P.add("pool", lambda e: e.affine_select(out=ident[:, :], in_=ident[:, :], pattern=[[-1, 128]],
                                            compare_op=ALU.is_equal, fill=0.0, base=0, channel_multiplier=1),
          reads=["ident"], writes=["ident"])
    P.add("pool", lambda e: e.memset(mlo[:, :], 1.0), writes=["mlo"])
    P.add("pool", lambda e: e.affine_select(out=mlo[:, :], in_=mlo[:, :], pattern=[[1, 64]],
                                            compare_op=ALU.is_ge, fill=0.0, base=0, channel_multiplier=-1),
          reads=["mlo"], writes=["mlo"])
    P.add("pool", lambda e: e.memset(mup[:, :], 1.0), writes=["mup"])
    P.add("pool", lambda e: e.affine_select(out=mup[:, :], in_=mup[:, :], pattern=[[-1, 64]],
                                            compare_op=ALU.is_gt, fill=0.0, base=0, channel_multiplier=1),
          reads=["mup"], writes=["mup"])
    P.add("pool", lambda e: e.memset(rmask[:, :], 1.0), writes=["rmask"])
    P.add("pool", lambda e: e.memset(rmask[:, :].rearrange("p (c t) -> p c t", t=64)[:, :, 0:1], 0.0),
          reads=["rmask"], writes=["rmask"])
    P.add("pool", lambda e: e.memset(S[:, :, :], 0.0), writes=["S0", "S1"])
    P.add("pool", lambda e: e.memset(Sbf[:, :, :], 0.0), writes=["Sbf0", "Sbf1"])
    P.dma("sp", "c0", out=cols[:, :], in_=cols_d[:, :], writes=["cols"])
    P.dma("sp", "c0", out=wgu[:, :], in_=wgu_d[:, :], writes=["wgu"])
    negb = sb("negb", [128, 2], F32)
    P.add("pool", lambda e: e.tensor_scalar(out=negb[:, :], in0=cols[:, 0:2], scalar1=-1.0, scalar2=1.0, op0=ALU.mult, op1=ALU.mult),
          reads=["cols"], writes=["negb"])

    c3 = lambda ap: ap.rearrange("p (c t) -> p c t", t=64)
    fin = []
    obank = [0]

    for g in range(NG):
        tok0 = g * GT
        P.dma("sp", "in_a", out=aT[:, :], in_=aT_d[:, tok0:tok0 + GT], writes=["aT"])
        P.dma("sp", "in_q", out=qT[:, :, :], in_=qT_d[:, tok0:tok0 + GT].rearrange("(dt p) t -> p dt t", p=128),
              writes=["qT"])
        P.dma("sp", "in_k", out=kT[:, :, :], in_=kT_d[:, tok0:tok0 + GT].rearrange("(dt p) t -> p dt t", p=128),
              writes=["kT"])
        P.dma("sp", "in_v", out=v64[:, :, :], in_=v_d[tok0:tok0 + GT, :].rearrange("(c s) v -> s c v", s=64),
              writes=["v64"])
        P.dma("sp", "in_g", out=gs[:, :, :], in_=gT_d[:, tok0:tok0 + GT].rearrange("(vt p) t -> p vt t", p=128),
              writes=["gs"])
        for vt in range(4):
            P.add("act", lambda e, vt=vt: e.activation(out=gs[:, vt, :], in_=gs[:, vt, :], func=AF.Silu),
                  reads=["gs"], writes=["gs"])
            P.add("pool", lambda e, vt=vt: e.tensor_scalar(out=gs[:, vt, :], in0=gs[:, vt, :],
                                                           scalar1=cols[:, 2 + vt:3 + vt], scalar2=1.0, op0=ALU.mult, op1=ALU.mult),
                  reads=["gs", "cols"], writes=["gs"])
        for dt in range(2):
            nch = [(i * 512, min(512, GT - i * 512)) for i in range((GT + 511) // 512)]
            for (o0, w) in nch:
                P.add("pe", lambda e, o0=o0, w=w, dt=dt: e.matmul(ps[:, B_Z, 0:w], lhsT=wgu[:, dt * 128:(dt + 1) * 128],
                                                                  rhs=aT[:, o0:o0 + w], start=True, stop=True),
                      reads=["wgu", "aT"], writes=["bZ"])
                P.add("act", lambda e, o0=o0, w=w, dt=dt: e.activation(out=et[:, o0:o0 + w], in_=ps[:, B_Z, 0:w], func=AF.Exp,
                                                                       scale=-1.0, bias=negb[:, dt:dt + 1]),
                      reads=["bZ", "negb"], writes=["et"])
            P.add("act", lambda e: e.activation(out=cc[:, :], in_=et[:, :], func=AF.Ln, scale=1.0, bias=1.0),
                  reads=["et"], writes=["cc"])
            P.add("dve", lambda e: e.tensor_tensor_scan(out=cc[:, :], data0=rmask[:, :], data1=cc[:, :], initial=0.0,
                                                        op0=ALU.mult, op1=ALU.add),
                  reads=["cc", "rmask"], writes=["cc"])
            P.add("dve", lambda e: e.tensor_tensor(out=c3(dd[:, :]), in0=c3(cc[:, :]),
                                                   in1=c3(cc[:, :])[:, :, 31:32].to_broadcast([128, GC, 64]),
                                                   op=ALU.subtract),
                  reads=["cc"], writes=["dd"])
            P.add("act", lambda e: e.activation(out=ep[:, :], in_=dd[:, :], func=AF.Exp, scale=-1.0 / 16),
                  reads=["dd"], writes=["ep"])
            P.add("act", lambda e: e.activation(out=em[:, :], in_=dd[:, :], func=AF.Exp, scale=1.0 / 16),
                  reads=["dd"], writes=["em"])
            P.add("act", lambda e: e.activation(out=al[:, :], in_=c3(cc[:, :])[:, :, 31], func=AF.Exp, scale=-1.0 / 16),
                  reads=["cc"], writes=["al"])
            P.add("act", lambda e, dt=dt: e.activation(out=dec[:, dt, :], in_=c3(cc[:, :])[:, :, 63], func=AF.Exp,
                                                       scale=-1.0 / 16),
                  reads=["cc"], writes=[("dec", dt)])
            P.add("dve", lambda e: e.tensor_tensor(out=tsm[:, :], in0=c3(cc[:, :])[:, :, 63], in1=c3(cc[:, :])[:, :, 31],
                                                   op=ALU.subtract),
                  reads=["cc"], writes=["tsm"])
            P.add("act", lambda e: e.activation(out=be[:, :], in_=tsm[:, :], func=AF.Exp, scale=-1.0 / 16),
                  reads=["tsm"], writes=["be"])
            P.add("dve", lambda e, dt=dt: e.scalar_tensor_tensor(out=qa[:, dt, :], in0=qT[:, dt, :], scalar=1.0 / 16,
                                                                 in1=ep[:, :], op0=ALU.mult, op1=ALU.mult),
                  reads=["qT", "ep"], writes=[("qa", dt)])
            P.add("dve", lambda e, dt=dt: e.scalar_tensor_tensor(out=qb[:, dt, :], in0=qT[:, dt, :], scalar=1.0 / 16,
                                                                 in1=em[:, :], op0=ALU.mult, op1=ALU.mult),
                  reads=["qT", "em"], writes=[("qb", dt)])
            P.add("dve", lambda e, dt=dt: e.tensor_tensor(out=ka[:, dt, :], in0=kT[:, dt, :], in1=em[:, :], op=ALU.mult),
                  reads=["kT", "em"], writes=[("ka", dt)])
            P.add("dve", lambda e, dt=dt: e.tensor_tensor(out=kb[:, dt, :], in0=kT[:, dt, :], in1=ep[:, :], op=ALU.mult),
                  reads=["kT", "ep"], writes=[("kb", dt)])
            P.add("dve", lambda e, dt=dt: e.tensor_tensor(out=c3(qi[:, dt, :]), in0=c3(qa[:, dt, :]),
                                                          in1=al[:, :].unsqueeze(2).to_broadcast([128, GC, 64]), op=ALU.mult),
                  reads=[("qa", dt), "al"], writes=[("qi", dt)])
            P.add("dve", lambda e, dt=dt: e.tensor_tensor(out=c3(ksT[:, dt, :]), in0=c3(ka[:, dt, :]),
                                                          in1=be[:, :].unsqueeze(2).to_broadcast([128, GC, 64]), op=ALU.mult),
                  reads=[("ka", dt), "be"], writes=[("ksT", dt)])
        for c0 in range(0, GC, 4):
            n = min(4, GC - c0)

            def tr(e, c0=c0, n=n):
                ins = None
                for j in range(n):
                    for dt in range(2):
                        ins = e.transpose(out=pst[0:64, (j * 2 + dt) * 128:(j * 2 + dt + 1) * 128],
                                          in_=ksT[:, dt, (c0 + j) * 64:(c0 + j + 1) * 64], identity=ident[:, :])
                return ins
            P.add("pe", tr, reads=[("ksT", 0), ("ksT", 1), "ident"], writes=["pst"])
            P.add("act", lambda e, c0=c0, n=n: e.activation(
                out=ks64[:, c0:c0 + n, :], in_=pst[0:64, 0:n * 256].rearrange("p (c d) -> p c d", d=256), func=AF.Copy),
                reads=["pst"], writes=["ks64"])
        for c0 in range(0, GC, 8):
            n = min(8, GC - c0)

            def scm(e, c0=c0, n=n):
                ins = None
                for j in range(n):
                    cs = slice((c0 + j) * 64, (c0 + j + 1) * 64)
                    for dt in range(2):
                        e.matmul(ps[0:64, B_Z, j * 64:(j + 1) * 64], lhsT=ka[:, dt, cs], rhs=qa[:, dt, cs],
                                 start=(dt == 0), stop=(dt == 1))
                    for dt in range(2):
                        ins = e.matmul(ps[0:64, B_UP, j * 64:(j + 1) * 64], lhsT=kb[:, dt, cs], rhs=qb[:, dt, cs],
                                       start=(dt == 0), stop=(dt == 1))
                return ins
            P.add("pe", scm, reads=[("ka", 0), ("ka", 1), ("qa", 0), ("qa", 1), ("kb", 0), ("kb", 1), ("qb", 0), ("qb", 1)],
                  writes=["bZ", "bUP"])
            P.add("dve", lambda e, n=n: e.tensor_tensor(out=t1[:, 0:n, :], in0=c3(ps[0:64, B_Z, 0:n * 64]),
                                                        in1=mlo[:, :].unsqueeze(1).to_broadcast([64, n, 64]), op=ALU.mult),
                  reads=["bZ", "mlo"], writes=["t1"])
            P.add("dve", lambda e, n=n: e.tensor_tensor(out=t2[:, 0:n, :], in0=c3(ps[0:64, B_UP, 0:n * 64]),
                                                        in1=mup[:, :].unsqueeze(1).to_broadcast([64, n, 64]), op=ALU.mult),
                  reads=["bUP", "mup"], writes=["t2"])
            P.add("pool", lambda e, c0=c0, n=n: e.tensor_tensor(out=sc[:, c0:c0 + n, :], in0=t1[:, 0:n, :], in1=t2[:, 0:n, :],
                                                                op=ALU.add),
                  reads=["t1", "t2"], writes=["sc"])
        def ubank(dt, c):
            return (B_U0 + dt) if c % 2 == 0 else (B_Z + dt)

        def ukey(dt, c):
            return ("bU", dt) if c % 2 == 0 else ("bZ" if dt == 0 else "bUP")

        def emit_u(c):
            for dt in range(2):
                P.add("pe", lambda e, c=c, dt=dt: e.matmul(ps[:, ubank(dt, c), :], lhsT=ks64[:, c, dt * 128:(dt + 1) * 128],
                                                           rhs=v64[:, c, :], start=True, stop=True),
                      reads=["ks64", "v64"], writes=[ukey(dt, c)])

        def emit_epilogue(blk, ob, okey):
            o3 = ps[:, ob, :].rearrange("p (v t) -> p v t", t=128)
            P.add("act", lambda e: e.activation(out=sq[:, :, :], in_=o3, func=AF.Square), reads=[okey], writes=["sq"])

            def stm(e):
                ins = None
                for vt in range(4):
                    ins = e.matmul(ps[:, B_ST, 0:128], lhsT=ones[:, :], rhs=sq[:, vt, :], start=(vt == 0), stop=(vt == 3))
                return ins
            P.add("pe", stm, reads=["sq", "ones"], writes=["bST"])
            P.add("act", lambda e: e.activation(out=rs[:, :], in_=ps[:, B_ST, 0:128], func=AF.Sqrt, scale=1.0 / DV, bias=EPS),
                  reads=["bST"], writes=["rs"])
            P.add("dve", lambda e: e.reciprocal(out=rs[:, :], in_=rs[:, :]), reads=["rs"], writes=["rs"])
            P.add("dve", lambda e: e.tensor_tensor(out=tmp[:, :, :], in0=o3, in1=rs[:, :].unsqueeze(1).to_broadcast([128, 4, 128]),
                                                   op=ALU.mult),
                  reads=[okey, "rs"], writes=["tmp"])
            bs = slice(blk * 128, (blk + 1) * 128)
            P.add("pool", lambda e: e.tensor_tensor(out=gs[:, :, bs], in0=tmp[:, :, :], in1=gs[:, :, bs], op=ALU.mult),
                  reads=["tmp", "gs"], writes=["gs"])

        emit_u(0)
        pending = None
        for blk in range(GB):
            ob = B_O0 + obank[0]
            obank[0] ^= 1
            okey = ("bO", ob)
            for h in range(2):
                c = blk * 2 + h
                cs = slice(c * 64, (c + 1) * 64)
                if c + 1 < GC:
                    emit_u(c + 1)

                def om(e, c=c, h=h, cs=cs, ob=ob):
                    ins = None
                    for vt in range(4):
                        o_ap = ps[:, ob, vt * 128 + h * 64: vt * 128 + h * 64 + 64]
                        e.matmul(o_ap, lhsT=v64[:, c, vt * 128:(vt + 1) * 128], rhs=sc[:, c, :], start=True, stop=False)
                        for dt in range(2):
                            ins = e.matmul(o_ap, lhsT=Sbf[:, dt, vt * 128:(vt + 1) * 128], rhs=qi[:, dt, cs],
                                           start=False, stop=(dt == 1))
                    return ins
                P.add("pe", om, reads=["v64", "sc", "Sbf0", "Sbf1", ("qi", 0), ("qi", 1)], writes=[okey])
                for dt in range(2):
                    P.add("dve", lambda e, c=c, dt=dt: e.scalar_tensor_tensor(out=S[:, dt, :], in0=S[:, dt, :],
                                                                              scalar=dec[:, dt, c:c + 1], in1=ps[:, ubank(dt, c), :],
                                                                              op0=ALU.mult, op1=ALU.add),
                          reads=[f"S{dt}", ("dec", dt), ukey(dt, c)], writes=[f"S{dt}"])
                    P.add("act", lambda e, dt=dt: e.activation(out=Sbf[:, dt, :], in_=S[:, dt, :], func=AF.Copy),
                          reads=[f"S{dt}"], writes=[f"Sbf{dt}"])
                if h == 0 and pending is not None:
                    emit_epilogue(*pending)
                    pending = None
            pending = (blk, ob, okey)
        emit_epilogue(*pending)
        gkeys = []
        for ci in range(tok0 // 528, (tok0 + GT - 1) // 528 + 1):
            lo = max(tok0, ci * 528)
            hi = min(tok0 + GT, (ci + 1) * 528)
            P.dma("sp", "out_g", out=og_d[ci * DV:(ci + 1) * DV, lo - ci * 528:hi - ci * 528].rearrange("(vt p) t -> p vt t", p=128),
                  in_=gs[:, :, lo - tok0:hi - tok0], reads=["gs"], writes=[("ogout", g, ci)])
            fin.append(("ogout", g, ci))
            gkeys.append(("ogout", g, ci))
        if on_store is not None:
            on_store(g, gkeys)
    return fin


L = 4224
NB = 33
HD = 128
NH = 4
EPS = 1e-6
PADK = 112


def sb_phase(nc, P, A, ps, qT_d, kT_d, v_d, gc_d, oT_d):
    pst = ps[:, 7, :].bitcast(BF16)
    BA, BB, BO = (0, 1), (2, 3), (4, 5)
    sb = lambda name, shape, dt: A.alloc(shape, dt)
    onesb = sb("onesb", [128, 128], BF16)
    onec = sb("onec", [128, 1], BF16)
    ntri = sb("ntri", [128, 128], BF16)
    mdiag = sb("mdiag", [128, 128], BF16)
    kval = sb("kval", [128, 1], F32)
    gc = sb("gc", [128, 2], F32)
    qr = sb("qr", [128, L], BF16)
    kr = sb("kr", [128, L], BF16)
    qn = sb("qn", [128, L], BF16)
    kn = sb("kn", [128, L], BF16)
    v_all = sb("v_all", [128, NB, NH * HD], BF16)
    o_h = sb("o_h", [128, NB, HD], BF16)
    oT_sb = sb("oT_sb", [128, NH, L], BF16)
    ident = sb("ident", [128, 128], BF16)
    sqb = [sb(f"sqb{i}", [128, 512], BF16) for i in range(2)]
    rst = [sb(f"rst{i}", [128, 512], F32) for i in range(2)]
    ebuf = [sb(f"e{i}", [128, 512], F32) for i in range(2)]
    spb = [sb(f"sp{i}", [128, 512], BF16) for i in range(2)]
    att = [sb(f"att{i}", [128, 512], BF16) for i in range(2)]
    dcy = [sb(f"dcy{i}", [128, 4], F32) for i in range(2)]
    acc = sb("acc", [128, 4, 128], F32)

    P.add("pool", lambda e: e.memset(onesb[:, :], 1.0), writes=["onesb"])
    P.add("pool", lambda e: e.memset(onec[:, :], 1.0), writes=["onec"])
    P.add("pool", lambda e: e.memset(ntri[:, :], -1.0), writes=["ntri"])
    P.add("pool", lambda e: e.affine_select(out=ntri[:, :], in_=ntri[:, :], pattern=[[-1, 128]], compare_op=ALU.is_ge,
                                            fill=0.0, base=0, channel_multiplier=1), reads=["ntri"], writes=["ntri"])
    P.add("pool", lambda e: e.memset(mdiag[:, :], 1.0), writes=["mdiag"])
    P.add("pool", lambda e: e.affine_select(out=mdiag[:, :], in_=mdiag[:, :], pattern=[[1, 128]], compare_op=ALU.is_gt,
                                            fill=0.0, base=0, channel_multiplier=-1), reads=["mdiag"], writes=["mdiag"])
    P.add("pool", lambda e: e.memset(kval[:, :], 1.0), writes=["kval"])
    P.add("pool", lambda e: e.affine_select(out=kval[:, :], in_=kval[:, :], pattern=[[0, 1]], compare_op=ALU.is_ge,
                                            fill=0.0, base=-PADK, channel_multiplier=1), reads=["kval"], writes=["kval"])
    P.add("pool", lambda e: e.memset(ident[:, :], 1.0), writes=["ident"])
    P.add("pool", lambda e: e.affine_select(out=ident[:, :], in_=ident[:, :], pattern=[[-1, 128]],
                                            compare_op=ALU.is_equal, fill=0.0, base=0, channel_multiplier=1),
          reads=["ident"], writes=["ident"])
    P.dma("sp", "c0", out=gc[:, :], in_=gc_d[:, :], writes=["gc"])
    P.dma("sp", "vin", out=v_all[:, :, :], in_=v_d[:, :].rearrange("(b s) d -> s b d", s=128), writes=["v_all"])

    par = [0]
    npar = [0]
    for h in range(NH):
        P.dma("sp", "qin", out=qr[:, :], in_=qT_d[h * HD:(h + 1) * HD, :], writes=["qr"])
        P.dma("sp", "kin", out=kr[:, :], in_=kT_d[h * HD:(h + 1) * HD, :], writes=["kr"])
        for (src, skey, dst, dkey, gi, sc_, bi_) in ((qr, "qr", qn, "qn", 0, 1.0, HD * EPS), (kr, "kr", kn, "kn", 1, 1.0 / HD, EPS)):
            for o0 in range(0, L, 512):
                w = min(512, L - o0)
                p = npar[0]
                npar[0] ^= 1
                P.add("dve", lambda e, src=src, o0=o0, w=w, p=p: e.tensor_tensor(out=sqb[p][:, 0:w], in0=src[:, o0:o0 + w],
                                                                                 in1=src[:, o0:o0 + w], op=ALU.mult),
                      reads=[skey], writes=[("sqb", p)])
                P.add("pe", lambda e, w=w, p=p: e.matmul(ps[:, BA[p], 0:w], lhsT=onesb[:, :], rhs=sqb[p][:, 0:w], start=True, stop=True),
                      reads=[("sqb", p), "onesb"], writes=[("bA", p)])
                P.add("act", lambda e, w=w, p=p, sc_=sc_, bi_=bi_: e.activation(out=rst[p][:, 0:w], in_=ps[:, BA[p], 0:w], func=AF.Sqrt,
                                                                                scale=sc_, bias=bi_),
                      reads=[("bA", p)], writes=[("rst", p)])
                P.add("dve", lambda e, w=w, p=p: e.reciprocal(out=rst[p][:, 0:w], in_=rst[p][:, 0:w]),
                      reads=[("rst", p)], writes=[("rst", p)])
                P.add("dve", lambda e, src=src, dst=dst, o0=o0, w=w, p=p, gi=gi: e.scalar_tensor_tensor(
                    out=dst[:, o0:o0 + w], in0=src[:, o0:o0 + w], scalar=gc[:, gi:gi + 1], in1=rst[p][:, 0:w],
                    op0=ALU.mult, op1=ALU.mult),
                      reads=[skey, "gc", ("rst", p)], writes=[dkey])
        units = []
        sblocks = [(0, 1)] + [(1 + 4 * k, min(5 + 4 * k, NB)) for k in range((NB - 1 + 3) // 4)]
        for I, (iq0, iq1) in enumerate(sblocks):
            for j in range(iq1):
                units.append((I, iq0, iq1, j, j == 0, j == iq1 - 1))

        def stage_a(u, p):
            (I, iq0, iq1, j, first, last) = u
            i0 = max(iq0, j)
            ncols = (iq1 - i0) * 128
            t0 = i0 * 128
            kj = kn[:, j * 128:(j + 1) * 128]
            qcols = qn[:, t0:t0 + ncols]
            diag = (j >= iq0)
            P.add("pe", lambda e: e.matmul(ps[:, BA[p], 0:ncols], lhsT=kj, rhs=qcols, start=True, stop=True),
                  reads=["kn", "qn"], writes=[("bA", p)])
            P.add("act", lambda e: e.activation(out=ebuf[p][:, 0:ncols], in_=ps[:, BA[p], 0:ncols], func=AF.Exp),
                  reads=[("bA", p)], writes=[("e", p)])
            P.add("act", lambda e: e.activation(out=spb[p][:, 0:ncols], in_=ebuf[p][:, 0:ncols], func=AF.Ln, scale=1.0, bias=1.0),
                  reads=[("e", p)], writes=[("sp", p)])
            if diag:
                P.add("pool", lambda e: e.tensor_tensor(out=spb[p][:, 0:128], in0=spb[p][:, 0:128], in1=mdiag[:, :], op=ALU.mult),
                      reads=[("sp", p), "mdiag"], writes=[("sp", p)])
            if j == 0:
                P.add("pool", lambda e: e.tensor_scalar(out=spb[p][:, 0:ncols], in0=spb[p][:, 0:ncols], scalar1=kval[:, 0:1], scalar2=1.0,
                                                        op0=ALU.mult, op1=ALU.mult),
                      reads=[("sp", p), "kval"], writes=[("sp", p)])

        def stage_b(u, p):
            (I, iq0, iq1, j, first, last) = u
            i0 = max(iq0, j)
            ncols = (iq1 - i0) * 128
            t0 = i0 * 128
            nqb = ncols // 128
            kj = kn[:, j * 128:(j + 1) * 128]
            qcols = qn[:, t0:t0 + ncols]
            diag = (j >= iq0)

            def mmB(e):
                e.matmul(ps[:, BB[p], 0:ncols], lhsT=kj, rhs=qcols, start=True, stop=False)
                e.matmul(ps[:, BB[p], 0:ncols], lhsT=ntri[:, :], rhs=spb[p][:, 0:ncols], start=False, stop=True)
                ins = None
                for il in range(nqb):
                    ins = e.matmul(ps[:, 6 + p, il:il + 1], lhsT=spb[p][:, il * 128:(il + 1) * 128], rhs=onec[:, :],
                                   start=True, stop=True)
                return ins
            ckey = ("bC", 0) if p == 0 else "pst"
            P.add("pe", mmB, reads=["kn", "qn", ("sp", p), "ntri", "onec"], writes=[("bB", p), ckey])
            P.add("act", lambda e: e.activation(out=att[p][:, 0:ncols], in_=ps[:, BB[p], 0:ncols], func=AF.Exp),
                  reads=[("bB", p)], writes=[("att", p)])
            P.add("act", lambda e: e.activation(out=dcy[p][:, 0:nqb], in_=ps[:, 6 + p, 0:nqb], func=AF.Exp, scale=-1.0),
                  reads=[ckey], writes=[("dcy", p)])
            if diag:
                P.add("pool", lambda e: e.tensor_tensor(out=att[p][:, 0:128], in0=att[p][:, 0:128], in1=mdiag[:, :], op=ALU.mult),
                      reads=[("att", p), "mdiag"], writes=[("att", p)])
            if j == 0:
                P.add("pool", lambda e: e.tensor_scalar(out=att[p][:, 0:ncols], in0=att[p][:, 0:ncols], scalar1=kval[:, 0:1], scalar2=1.0,
                                                        op0=ALU.mult, op1=ALU.mult),
                      reads=[("att", p), "kval"], writes=[("att", p)])

        def stage_o(u, p, h=h):
            (I, iq0, iq1, j, first, last) = u
            i0 = max(iq0, j)
            nqb = iq1 - i0
            nq = iq1 - iq0
            if first:
                P.add("pool", lambda e: e.memset(acc[:, 0:nq, :], 0.0), writes=["acc"])

            def mmO(e):
                ins = None
                for il in range(nqb):
                    ins = e.matmul(ps[:, BO[p], il * 128:(il + 1) * 128], lhsT=att[p][:, il * 128:(il + 1) * 128],
                                   rhs=v_all[:, j, h * HD:(h + 1) * HD], start=True, stop=True)
                return ins
            P.add("pe", mmO, reads=[("att", p), "v_all"], writes=[("bO", p)])
            for il in range(nqb):
                ia = (i0 - iq0) + il
                P.add("dve", lambda e, il=il, ia=ia: e.scalar_tensor_tensor(
                    out=acc[:, ia, :], in0=acc[:, ia, :], scalar=dcy[p][:, il:il + 1], in1=ps[:, BO[p], il * 128:(il + 1) * 128],
                    op0=ALU.mult, op1=ALU.add),
                      reads=["acc", ("dcy", p), ("bO", p)], writes=["acc"])
            if last:
                P.add("act", lambda e: e.activation(out=o_h[:, iq0:iq1, :], in_=acc[:, 0:nq, :], func=AF.Copy),
                      reads=["acc"], writes=["o_h"])

        nu = len(units)
        base = par[0]
        for step in range(nu + 2):
            if step < nu:
                stage_a(units[step], (base + step) % 2)
            if 1 <= step <= nu:
                stage_b(units[step - 1], (base + step - 1) % 2)
            if step >= 2:
                stage_o(units[step - 2], (base + step - 2) % 2)
        par[0] = (base + nu) % 2
        for b0 in range(0, NB, 8):
            n = min(8, NB - b0)

            def tro(e, b0=b0, n=n):
                ins = None
                for jj in range(n):
                    ins = e.transpose(out=pst[:, jj * 128:(jj + 1) * 128], in_=o_h[:, b0 + jj, :], identity=ident[:, :])
                return ins
            P.add("pe", tro, reads=["o_h", "ident"], writes=["pst"])
            P.add("dve", lambda e, b0=b0, n=n, h=h: e.tensor_copy(out=oT_sb[:, h, b0 * 128:(b0 + n) * 128], in_=pst[:, 0:n * 128]),
                  reads=["pst"], writes=["oT_sb"])
    fin = []
    for ci in range(8):
        P.dma("sp", "oout", out=oT_d[ci * 512:(ci + 1) * 512, :].rearrange("(h p) t -> p h t", p=128),
              in_=oT_sb[:, :, ci * 528:(ci + 1) * 528], reads=["oT_sb"], writes=[("oout", ci)])
        fin.append(("oout", ci))
    return fin


T = 1056
TG = 352
NTG = 3
D = 2048
KT_D = 16
DFF = 5632
SCS = [12, 12, 12, 8]
TTILES = [(i * 128, 128) for i in range(8)] + [(1024, 32)]


class DenseCtx:
    def __init__(self, nc, P, A, ps, wb=256, n_wslots=4):
        self.nc = nc
        self.P = P
        self.psum = ps
        self.wb = wb
        self.wslots = [A.alloc([128, KT_D, wb], BF16) for i in range(n_wslots)]
        self.wi = 0
        self.ones = A.alloc([128, 128], BF16)
        self.sq = [A.alloc([128, T], BF16) for i in range(2)]
        self.sqi = 0
        self.rstd = A.alloc([128, T], F32)
        self.tm = 0
        P.add("pool", lambda e: e.memset(self.ones[:, :], 1.0), writes=[("ones",)])

    def tri_view(self, t):
        return self.psum[:, 3 * t:3 * t + 3, 0:TG]

    def tri_key(self, t):
        return ("ps", t)

    def next_wslot(self):
        i = self.wi
        self.wi = (self.wi + 1) % len(self.wslots)
        return i

    def next_tm_bank(self):
        b = 6 + self.tm
        self.tm ^= 1
        return b


def v3(ap):
    return ap.rearrange("p (g t) -> p g t", g=NTG)


def rms_stats(C, hT, hkey):
    P = C.P
    t = P.next_tri()
    for kt in range(KT_D):
        si = C.sqi
        C.sqi ^= 1
        sq = C.sq[si]
        P.add("act", lambda e, sq=sq, kt=kt: e.activation(out=sq[:, :], in_=hT[:, kt, :], func=AF.Square),
              reads=[hkey(kt)], writes=[("sq", si)])

        def mm(e, sq=sq, kt=kt, t=t):
            ins = None
            for g in range(NTG):
                ins = e.matmul(C.psum[:, 3 * t + g, 0:TG], lhsT=C.ones[:, :], rhs=sq[:, g * TG:(g + 1) * TG],
                               start=(kt == 0), stop=(kt == KT_D - 1))
            return ins
        P.add("pe", mm, reads=[("sq", si), ("ones",)], writes=[C.tri_key(t)])
    P.add("act", lambda e: e.activation(out=v3(C.rstd[:, :]), in_=C.tri_view(t), func=AF.Sqrt, scale=1.0 / D, bias=1e-6),
          reads=[C.tri_key(t)], writes=[("rstd",)])
    P.add("dve", lambda e: e.reciprocal(out=C.rstd[:, :], in_=C.rstd[:, :]), reads=[("rstd",)], writes=[("rstd",)])


def apply_norm(C, hT, hkey, gcol, gkey, yT, ykey):
    P = C.P
    for kt in range(KT_D):
        sc_ = 1.0 if gcol is None else gcol[:, kt:kt + 1]
        rd = [hkey(kt), ("rstd",)] + ([] if gcol is None else [gkey])
        P.add("dve", lambda e, kt=kt, sc_=sc_: e.scalar_tensor_tensor(out=yT[:, kt, :], in0=hT[:, kt, :], scalar=sc_, in1=C.rstd[:, :],
                                                                      op0=ALU.mult, op1=ALU.mult),
              reads=rd, writes=[ykey(kt)])


def load_wblock(C, w_dram, row0, KT, c0, cw):
    ws = C.next_wslot()
    wt = C.wslots[ws]
    src = w_dram[row0:row0 + KT * 128, c0:c0 + cw].rearrange("(kt p) n -> p kt n", p=128)
    C.P.dma("pool", f"w{ws}", out=wt[:, 0:KT, 0:cw], in_=src, writes=[("w", ws)])
    return ws


def mm_group(C, wt, wkey, j, rows, xT, xkeys, KT):
    P = C.P
    t = P.next_tri()

    def mm(e):
        ins = None
        for kt in range(KT):
            for g in range(NTG):
                ins = e.matmul(C.psum[0:rows, 3 * t + g, 0:TG], lhsT=wt[:, kt, j * 128:j * 128 + rows],
                               rhs=xT(kt)[:, g * TG:(g + 1) * TG], start=(kt == 0), stop=(kt == KT - 1))
        return ins
    P.add("pe", mm, reads=[wkey] + xkeys, writes=[C.tri_key(t)])
    return t


def dense(C, xT, xkey, KT, w_dram, row0, col0, ncols, epilogue):
    wb = C.wb
    nt = 0
    xkeys = [xkey(kt) for kt in range(KT)]
    for b in range((ncols + wb - 1) // wb):
        c0 = col0 + b * wb
        cw = min(wb, col0 + ncols - c0)
        ws = load_wblock(C, w_dram, row0, KT, c0, cw)
        for j in range((cw + 127) // 128):
            rows = min(128, cw - j * 128)
            t = mm_group(C, C.wslots[ws], ("w", ws), j, rows, xT, xkeys, KT)
            epilogue(nt, rows, t)
            nt += 1


def dense_tm(C, yT, ykey, wts, wkeys, epilogue):
    P = C.P
    ykeys = [ykey(kt) for kt in range(KT_D)]
    for tt, (t0, rows) in enumerate(TTILES):
        bank = C.next_tm_bank()

        def mm(e, t0=t0, rows=rows, bank=bank):
            ins = None
            for bi, wt in enumerate(wts):
                for kt in range(KT_D):
                    ins = e.matmul(C.psum[0:rows, bank, bi * 256:(bi + 1) * 256], lhsT=yT[:, kt, t0:t0 + rows], rhs=wt[:, kt, 0:256],
                                   start=(kt == 0), stop=(kt == KT_D - 1))
            return ins
        P.add("pe", mm, reads=list(wkeys) + ykeys, writes=[("tmb", bank)])
        epilogue(tt, t0, rows, bank)


def load_tokens(P, slot, dst, src_ap, keyf):
    for q in range(4):
        P.dma("sp", slot, out=dst[:, 4 * q:4 * q + 4, :], in_=src_ap(512 * q, 512 * (q + 1)).rearrange("(kt p) t -> p kt t", p=128),
              writes=[keyf(kt) for kt in range(4 * q, 4 * q + 4)])


hkey = lambda kt: ("h", kt)
ykey = lambda kt: ("y", kt)
ukey = lambda kt: ("u", kt)


def inproj_phase(nc, P, A, ps, xbT_d, w_d, g_d, qT_s, kT_s, gT_s, v_s, aT_s):
    C = DenseCtx(nc, P, A, ps)
    hT = A.alloc([128, KT_D, T], F32)
    yT = A.alloc([128, KT_D, T], BF16)
    gcol = A.alloc([128, 16], F32)
    outs = [A.alloc([128, T], BF16) for i in range(4)]
    vst = [A.alloc([128, 512], BF16) for i in range(2)]
    a_sb = A.alloc([16, T], F32)
    P.dma("sp", "g", out=gcol[:, :], in_=g_d[:, :], writes=[("g",)])
    st = {"i": 0, "v": 0}
    fin = []
    load_tokens(P, "hin", hT, lambda r0, r1: xbT_d[r0:r1, 0:T], hkey)
    for tg in range(4):
        c0 = tg * T
        rms_stats(C, hT, hkey)
        apply_norm(C, hT, hkey, gcol, ("g",), yT, ykey)
        if tg < 3:
            load_tokens(P, "hin", hT, lambda r0, r1, c1=c0 + T: xbT_d[r0:r1, c1:c1 + T], hkey)

        def epi_fm(dst, row0, tag):
            def epi(nt, rows, t):
                i = st["i"] % 4
                st["i"] += 1
                o = outs[i]
                if st["i"] % 2 == 0:
                    P.add("act", lambda e, o=o, t=t: e.activation(out=v3(o[:, :]), in_=C.tri_view(t), func=AF.Copy),
                          reads=[C.tri_key(t)], writes=[("o", i)])
                else:
                    P.add("dve", lambda e, o=o, t=t: e.tensor_copy(out=v3(o[:, :]), in_=C.tri_view(t)),
                          reads=[C.tri_key(t)], writes=[("o", i)])
                P.dma("sp", f"o{i}", out=dst[row0 + nt * 128:row0 + (nt + 1) * 128, c0:c0 + T], in_=o[:, :], reads=[("o", i)],
                      writes=[(tag, tg, nt)])
                fin.append((tag, tg, nt))
            return epi
        dense(C, lambda kt: yT[:, kt, :], ykey, KT_D, w_d, 0, 0, 256, epi_fm(qT_s, 0, "qs"))
        dense(C, lambda kt: yT[:, kt, :], ykey, KT_D, w_d, 0, 256, 256, epi_fm(kT_s, 0, "ks"))
        dense(C, lambda kt: yT[:, kt, :], ykey, KT_D, w_d, 0, 1024, 512, epi_fm(gT_s, 0, "gs"))

        def epi_a(nt, rows, t):
            P.add("dve", lambda e, t=t: e.tensor_copy(out=v3(a_sb[:, :]), in_=C.psum[0:16, 3 * t:3 * t + 3, 0:TG]),
                  reads=[C.tri_key(t)], writes=[("a",)])
            P.dma("sp", "aout", out=aT_s[:, c0:c0 + T], in_=a_sb[:, :], reads=[("a",)], writes=[("as", tg)])
            fin.append(("as", tg))
        dense(C, lambda kt: yT[:, kt, :], ykey, KT_D, w_d, 0, 1536, 16, epi_a)
        ws0 = load_wblock(C, w_d, 0, KT_D, 512, 256)
        ws1 = load_wblock(C, w_d, 0, KT_D, 768, 256)

        def epi_v(tt, t0, rows, bank):
            i = st["v"]
            st["v"] ^= 1
            P.add("act", lambda e, i=i, rows=rows, bank=bank: e.activation(out=vst[i][0:rows, :], in_=C.psum[0:rows, bank, :], func=AF.Copy),
                  reads=[("tmb", bank)], writes=[("vst", i)])
            P.dma("sp", f"vs{i}", out=v_s[c0 + t0:c0 + t0 + rows, :], in_=vst[i][0:rows, :], reads=[("vst", i)], writes=[("vs", tg, tt)])
            fin.append(("vs", tg, tt))
        dense_tm(C, yT, ykey, [C.wslots[ws0], C.wslots[ws1]], [("w", ws0), ("w", ws1)], epi_v)
    return fin


def ffn_phase(nc, P, A, ps, kind, h_src, o_all_d, wmix_d, wg_d, wu_d, wd_d, g_d, hout_d, xh_loc=None):
    C = DenseCtx(nc, P, A, ps)
    hT = A.alloc([128, KT_D, T], F32)
    yT = A.alloc([128, KT_D, T], BF16)
    uT = A.alloc([128, 12, T], BF16)
    gcols = A.alloc([128, 16], F32)
    sg = [A.alloc([128, T], F32) for i in range(2)]
    P.dma("sp", "g", out=gcols[:, :], in_=g_d[:, :], writes=[("g",)])
    for q in range(4):
        for half in range(2):
            def ld(e, q=q, half=half):
                if "rowoff" not in P.cache:
                    P.cache["rowoff"] = e.snap((e.partition_id() % 4) * 4096)
                off = P.cache["rowoff"]
                return e.dma_start(out=yT[:, 4 * q:4 * q + 4, half * 528:(half + 1) * 528],
                                   in_=o_all_d[bass.ds(off + (half * 2048 + q * 512), 512), :].rearrange("(kt p) t -> p kt t", p=128))
            P.add("sp", ld, reads=[(("o_all",), ci) for ci in range(8)], writes=[ykey(kt) for kt in range(4 * q, 4 * q + 4)], dma="oin")
    load_tokens(P, "hin", hT, lambda r0, r1: h_src[r0:r1, :], hkey)

    def epi_resid(nt, rows, t):
        P.add("dve", lambda e, nt=nt, t=t: e.tensor_tensor(out=v3(hT[:, nt, :]), in0=v3(hT[:, nt, :]), in1=C.tri_view(t), op=ALU.add),
              reads=[C.tri_key(t), hkey(nt)], writes=[hkey(nt)])
    dense(C, lambda kt: yT[:, kt, :], ykey, KT_D, wmix_d, 0, 0, D, epi_resid)
    rms_stats(C, hT, hkey)
    apply_norm(C, hT, hkey, gcols, ("g",), yT, ykey)
    sgi = [0]
    f_nt = 0
    xkeys = [ykey(kt) for kt in range(KT_D)]
    for sc_n in SCS:
        for blk in range(sc_n // 2):
            c0 = (f_nt + 2 * blk) * 128
            wsg = load_wblock(C, wg_d, 0, KT_D, c0, 256)
            wsu = load_wblock(C, wu_d, 0, KT_D, c0, 256)
            for j in range(2):
                jj = 2 * blk + j
                tg_ = mm_group(C, C.wslots[wsg], ("w", wsg), j, 128, lambda kt: yT[:, kt, :], xkeys, KT_D)
                si = sgi[0]
                sgi[0] ^= 1
                P.add("act", lambda e, si=si, t=tg_: e.activation(out=v3(sg[si][:, :]), in_=C.tri_view(t), func=AF.Silu),
                      reads=[C.tri_key(tg_)], writes=[("sg", si)])
                tu_ = mm_group(C, C.wslots[wsu], ("w", wsu), j, 128, lambda kt: yT[:, kt, :], xkeys, KT_D)
                P.add("dve", lambda e, si=si, t=tu_, jj=jj: e.tensor_tensor(out=v3(uT[:, jj, :]), in0=v3(sg[si][:, :]), in1=C.tri_view(t),
                                                                            op=ALU.mult),
                      reads=[C.tri_key(tu_), ("sg", si)], writes=[ukey(jj)])
        dense(C, lambda kt: uT[:, kt, :], ukey, sc_n, wd_d, f_nt * 128, 0, D, epi_resid)
        f_nt += sc_n
    fin = []
    for q in range(4):
        P.dma("sp", "hout", out=hout_d[512 * q:512 * (q + 1), :].rearrange("(kt p) t -> p kt t", p=128), in_=hT[:, 4 * q:4 * q + 4, :],
              reads=[hkey(kt) for kt in range(4 * q, 4 * q + 4)], writes=[("hout", q)])
        fin.append(("hout", q))
    if kind == "A":
        rms_stats(C, hT, hkey)
        apply_norm(C, hT, hkey, None, None, yT, ykey)
        for q in range(4):
            P.dma("sp", "xh", out=xh_loc[512 * q:512 * (q + 1), :].rearrange("(kt p) t -> p kt t", p=128), in_=yT[:, 4 * q:4 * q + 4, :],
                  reads=[ykey(kt) for kt in range(4 * q, 4 * q + 4)], writes=[("xh", q)])
            fin.append(("xh", q))
    return fin


def qkv_phase(nc, P, A, ps, xh_all_d, w_d, g_d, q4T_s, k4T_s, v4_s):
    C = DenseCtx(nc, P, A, ps, n_wslots=6)
    yT = A.alloc([128, KT_D, T], BF16)
    gcols = A.alloc([128, 32], F32)
    outs = [A.alloc([128, T], BF16) for i in range(4)]
    vst = [A.alloc([128, 512], BF16) for i in range(2)]
    P.dma("sp", "g", out=gcols[:, :], in_=g_d[:, :], writes=[("g",)])
    for b in range(6):
        ws = load_wblock(C, w_d, 0, KT_D, b * 256, 256)
        assert ws == b
        goff = 16 if b < 2 else 0
        P.add("pool", lambda e, b=b, goff=goff: e.tensor_tensor(out=C.wslots[b][:, :, :], in0=C.wslots[b][:, :, :],
                                                                in1=gcols[:, goff:goff + 16].unsqueeze(2).to_broadcast([128, 16, 256]),
                                                                op=ALU.mult),
              reads=[("w", b), ("g",)], writes=[("w", b)])
    st = {"i": 0, "v": 0}
    fin = []
    xkeys = [ykey(kt) for kt in range(KT_D)]
    for tg in range(4):
        c0 = tg * T
        for ci in range(8):
            P.dma("sp", "xin", out=yT[:, 2 * ci:2 * ci + 2, :],
                  in_=xh_all_d[ci * 1024 + tg * 256:ci * 1024 + (tg + 1) * 256, :].rearrange("(kt p) t -> p kt t", p=128),
                  reads=[(("xh_all",), c8) for c8 in range(8)], writes=[ykey(2 * ci), ykey(2 * ci + 1)])
        for (dst, tag, blocks) in ((q4T_s, "q4", (0, 1)), (k4T_s, "k4", (2, 3))):
            for bi, b in enumerate(blocks):
                for j in range(2):
                    nt = bi * 2 + j
                    t = mm_group(C, C.wslots[b], ("w", b), j, 128, lambda kt: yT[:, kt, :], xkeys, KT_D)
                    i = st["i"] % 4
                    st["i"] += 1
                    o = outs[i]
                    if st["i"] % 2 == 0:
                        P.add("act", lambda e, o=o, t=t: e.activation(out=v3(o[:, :]), in_=C.tri_view(t), func=AF.Copy),
                              reads=[C.tri_key(t)], writes=[("o", i)])
                    else:
                        P.add("dve", lambda e, o=o, t=t: e.tensor_copy(out=v3(o[:, :]), in_=C.tri_view(t)),
                              reads=[C.tri_key(t)], writes=[("o", i)])
                    P.dma("sp", f"o{i}", out=dst[nt * 128:(nt + 1) * 128, c0:c0 + T], in_=o[:, :], reads=[("o", i)], writes=[(tag, tg, nt)])
                    fin.append((tag, tg, nt))

        def epi_v(tt, t0, rows, bank):
            i = st["v"]
            st["v"] ^= 1
            P.add("act", lambda e, i=i, rows=rows, bank=bank: e.activation(out=vst[i][0:rows, :], in_=C.psum[0:rows, bank, :], func=AF.Copy),
                  reads=[("tmb", bank)], writes=[("vst", i)])
            P.dma("sp", f"vs{i}", out=v4_s[c0 + t0:c0 + t0 + rows, :], in_=vst[i][0:rows, :], reads=[("vst", i)], writes=[("v4", tg, tt)])
            fin.append(("v4", tg, tt))
        dense_tm(C, yT, ykey, [C.wslots[4], C.wslots[5]], [("w", 4), ("w", 5)], epi_v)
    return fin


def build_fused(stop=99):
    nc = bass.Bass("TRN2", target_bir_lowering=False)
    dt_in = lambda name, shape, dt=F32: nc.dram_tensor(name, shape, dt, kind="ExternalInput")
    scr = lambda name, shape, dt: nc.dram_tensor(name, shape, dt)
    RG = [[0, 1, 2, 3], [4, 5, 6, 7]]
    P = Prog(nc)
    A = Arena(nc, 204 * 1024)
    ps = nc.alloc_psum_tensor("ps", [128, 8, 512], F32)

    def gather(slot, src, dst, rows, rkeys, wkey):
        for ci in range(8):
            P.coll(slot, lambda e, ci=ci: e.collective_compute("AllGather", ALU.bypass, replica_groups=RG,
                                                               ins=[src[ci * rows:(ci + 1) * rows, :]],
                                                               outs=[dst[ci * 4 * rows:(ci + 1) * 4 * rows, :]]),
                   reads=rkeys, writes=[(wkey, ci)])

    def finish(keys, tap=None):
        if tap is not None:
            src, shape, dt = tap
            dbg = nc.dram_tensor("dbg", shape, dt, kind="ExternalOutput")
            P.dma("sp", "dbg", out=dbg.ap(), in_=(src if hasattr(src, "tensor") else src.ap()), reads=keys, writes=[("dbg",)])
            keys = [("dbg",)]
        for en in ("sp", "pool", "act", "dve", "pe"):
            P.add(en, None, reads=keys)
        P.build()
        return nc, P

    xbT_d = dt_in("xbT", [D, L])
    w_in_d = dt_in("w_in_h", [D, 1552])
    gA_d = dt_in("gcolA", [128, 16])
    wgu_d = dt_in("wgu", [16, DK])
    colsA_d = dt_in("colsA", [128, 8])
    qT_s = scr("qT_s", [DK, L], BF16); kT_s = scr("kT_s", [DK, L], BF16); gT_s = scr("gT_s", [DV, L], BF16)
    v_s = scr("v_s", [L, DV], BF16); aT_s = scr("aT_s", [16, L], F32)
    og_loc = scr("og_loc", [8 * DV, 528], BF16); og_all = scr("og_all", [8 * 4 * DV, 528], BF16)
    fin = inproj_phase(nc, P, A, ps, xbT_d, w_in_d, gA_d, qT_s, kT_s, gT_s, v_s, aT_s)
    if stop == 0:
        return finish(fin, (gT_s, [DV, L], BF16))
    P.barrier(); A.reset(0)
    og_state = {"next": 0, "keys": {}}

    def on_store(g, keys):
        if stop == 1:
            return
        for k in keys:
            og_state["keys"].setdefault(k[2], []).append(k)
        upto = ((g + 1) * GT) // 528
        for ci in range(og_state["next"], upto):
            P.coll("cc1", lambda e, ci=ci: e.collective_compute("AllGather", ALU.bypass, replica_groups=RG,
                                                                ins=[og_loc[ci * 512:(ci + 1) * 512, :]],
                                                                outs=[og_all[ci * 2048:(ci + 1) * 2048, :]]),
                   reads=og_state["keys"][ci], writes=[(("o_all",), ci)])
        og_state["next"] = upto
    fin = gla_phase(nc, P, A, ps, qT_s, kT_s, v_s, gT_s, aT_s, wgu_d, colsA_d, og_loc, on_store)
    if stop == 1:
        return finish(fin, (og_loc, [8 * DV, 528], BF16))
    assert og_state["next"] == 8
    if stop == 2:
        return finish([(("o_all",), ci) for ci in range(8)], (og_all[3 * DV:4 * DV, :], [DV, 528], BF16))
    P.barrier(); A.reset(0)
    hT_d = dt_in("hT", [D, T])
    w_out_d = dt_in("w_out", [D, D])
    wg0_d = dt_in("w_gate0", [D, DFF]); wu0_d = dt_in("w_up0", [D, DFF]); wd0_d = dt_in("w_down0", [DFF, D])
    gF0_d = dt_in("gF0", [128, 16])
    h2_s = scr("h2_s", [D, T], F32)
    xh_loc = scr("xh_loc", [D, T], BF16); xh_all = scr("xh_all", [8 * 4 * 256, T], BF16)
    fin = ffn_phase(nc, P, A, ps, "A", hT_d, og_all, w_out_d, wg0_d, wu0_d, wd0_d, gF0_d, h2_s, xh_loc)
    if stop == 3:
        return finish(fin, (h2_s, [D, T], F32))
    gather("cc2", xh_loc, xh_all, 256, fin, ("xh_all",))
    P.barrier(); A.reset(0)
    w_qkv_d = dt_in("w_qkv_h", [D, 1536])
    gQKV_d = dt_in("gQKV", [128, 32])
    gqk_d = dt_in("gqk", [128, 2])
    q4T_s = scr("q4T_s", [512, L], BF16); k4T_s = scr("k4T_s", [512, L], BF16); v4_s = scr("v4_s", [L, 512], BF16)
    o_loc = scr("o_loc", [8 * 512, 528], BF16); o_all = scr("o_all", [8 * 4 * 512, 528], BF16)
    fin = qkv_phase(nc, P, A, ps, xh_all, w_qkv_d, gQKV_d, q4T_s, k4T_s, v4_s)
    if stop == 4:
        return finish(fin, (k4T_s, [512, L], BF16))
    P.barrier(); A.reset(0)
    fin = sb_phase(nc, P, A, ps, q4T_s, k4T_s, v4_s, gqk_d, o_loc)
    if stop == 5:
        return finish(fin, (o_loc, [8 * 512, 528], BF16))
    gather("cc3", o_loc, o_all, 512, fin, ("o_all",))
    P.barrier(); A.reset(0)
    w_o_d = dt_in("w_o", [D, D])
    wg1_d = dt_in("w_gate1", [D, DFF]); wu1_d = dt_in("w_up1", [D, DFF]); wd1_d = dt_in("w_down1", [DFF, D])
    gF1_d = dt_in("gF1", [128, 16])
    out_d = nc.dram_tensor("hout", [D, T], F32, kind="ExternalOutput")
    fin = ffn_phase(nc, P, A, ps, "B", h2_s, o_all, w_o_d, wg1_d, wu1_d, wd1_d, gF1_d, out_d)
    return finish(fin)


_PROG = []
_STOP = [99]


def _cols(vec):
    v = np.asarray(vec, dtype=np.float32)
    return np.ascontiguousarray(v.reshape(-1, 128).T)


def kernel(x, meta_tokens, g_norm_a, w_in_a, w_gate_up_a, b_gate_a, g_onorm_a, w_out_a,
           g_kv_norm, w_kv, g_k, g_norm_b, w_q_b, g_q_b, w_o_b,
           g_ffn_norm, w_ffn_gate, w_ffn_up, w_ffn_down):
    f32 = np.float32
    C = np.ascontiguousarray
    x = np.asarray(x, f32)
    B = x.shape[0]
    meta = np.asarray(meta_tokens, f32)
    h0 = np.concatenate([np.zeros((B, 112, D), f32), np.broadcast_to(meta[None], (B, 16, D)), x], axis=1)
    xbT = [C(h0[b].T) for b in range(B)]
    w_in = np.asarray(w_in_a, f32)[0]
    wgu_full = np.asarray(w_gate_up_a, f32)[0]
    bg = np.asarray(b_gate_a, f32)[0]
    gon = np.asarray(g_onorm_a, f32)[0]
    wkv = np.asarray(w_kv, f32)
    wq = np.asarray(w_q_b, f32)[0]
    gffn = np.asarray(g_ffn_norm, f32)
    shared = {
        "gcolA": _cols(np.asarray(g_norm_a)[0]),
        "w_out": C(np.asarray(w_out_a, f32)[0]),
        "w_gate0": C(np.asarray(w_ffn_gate, f32)[0]), "w_up0": C(np.asarray(w_ffn_up, f32)[0]), "w_down0": C(np.asarray(w_ffn_down, f32)[0]),
        "gF0": _cols(gffn[0]),
        "gQKV": C(np.concatenate([_cols(np.asarray(g_kv_norm)), _cols(np.asarray(g_norm_b)[0])], axis=1)),
        "gqk": C(np.stack([np.asarray(g_q_b, f32)[0], np.asarray(g_k, f32)], axis=1)),
        "w_o": C(np.asarray(w_o_b, f32)[0]),
        "w_gate1": C(np.asarray(w_ffn_gate, f32)[1]), "w_up1": C(np.asarray(w_ffn_up, f32)[1]), "w_down1": C(np.asarray(w_ffn_down, f32)[1]),
        "gF1": _cols(gffn[1]),
    }
    in_maps = []
    for c in range(8):
        b, i = divmod(c, 4)
        m = dict(shared)
        m["xbT"] = xbT[b]
        m["hT"] = C(xbT[b][:, i * T:(i + 1) * T])
        m["w_in_h"] = C(np.concatenate([w_in[:, i * DK:(i + 1) * DK], w_in[:, 1024 + i * DK:1024 + (i + 1) * DK],
                                        w_in[:, 2048 + i * DV:2048 + (i + 1) * DV], w_in[:, 4096 + i * DV:4096 + (i + 1) * DV],
                                        w_in[:, 6144:6160]], axis=1))
        m["wgu"] = C(wgu_full[:, i * DK:(i + 1) * DK])
        cols = np.zeros((128, 8), f32)
        cols[:, 0:2] = bg[i * DK:(i + 1) * DK].reshape(2, 128).T
        cols[:, 2:6] = gon.reshape(4, 128).T
        m["colsA"] = cols
        m["w_qkv_h"] = C(np.concatenate([wq[:, i * 512:(i + 1) * 512], wkv[:, i * 512:(i + 1) * 512],
                                         wkv[:, 2048 + i * 512:2048 + (i + 1) * 512]], axis=1))
        in_maps.append(m)
    if _STOP[0] != 99:
        return in_maps
    if not _PROG:
        _PROG.append(build_fused()[0])
    res = run_bass_kernel_spmd(_PROG[0], in_maps, core_ids=list(range(8)))
    r = res.results
    hfin = np.concatenate([np.asarray(r[c]["hout"]).T for c in range(8)], axis=0).reshape(B, L, D)
    return np.ascontiguousarray(hfin[:, 128:, :].astype(np.float32))
```
